# Optimizing a Trainium2 kernel written in Bass

```python
import math
import jax, jax.numpy as jnp
from jax import lax
import numpy as np

D_MODEL = 1024
BATCH = 8
SEQ = 4096
DEPTH = 4

GRID_W = 64
CTX_LEN = 256
HEAD_DIM = 64
M_HEADS = 4
M_WIDTH = M_HEADS * HEAD_DIM
NA_HEADS = 8
NA_WIDTH = NA_HEADS * HEAD_DIM
LRU_WIDTH = D_MODEL - M_WIDTH - NA_WIDTH
LRU_BLOCKS = 4
LRU_BLOCK = LRU_WIDTH // LRU_BLOCKS
N_GATES = 4 * M_HEADS
M_COLS = 4 * M_WIDTH + N_GATES
NA_COLS = 3 * NA_WIDTH
LRU_COLS = 2 * LRU_WIDTH
IN_COLS = M_COLS + NA_COLS + LRU_COLS
MLP_HIDDEN = 4 * D_MODEL
CHUNK = 64
WIN_R_MAX = 8
WIN_C = 16
RPB_R = 2 * WIN_R_MAX - 1
RPB_C = 2 * WIN_C - 1
CONV_W = 4
CONV_LEFT = 2
LRU_C = 8.0
ROPE_BASE = 10000.0
EPS = 1e-6

kernel_name = 'hybrid_mlstm_natten_rglru_dit_block'


def rmsnorm(x, g):
    xf = x.astype(jnp.float32)
    y = xf * lax.rsqrt(jnp.mean(xf * xf, axis=-1, keepdims=True) + EPS)
    return (y * g.astype(jnp.float32)).astype(x.dtype)


def modulate(h, shift, scale):
    return h * (1 + scale[:, None]) + shift[:, None]


def axial_rope(n):
    t = jnp.arange(n)
    row = (t // GRID_W).astype(jnp.float32)
    col = (t % GRID_W).astype(jnp.float32)
    nf = HEAD_DIM // 4
    inv = ROPE_BASE ** (-jnp.arange(nf, dtype=jnp.float32) / nf)
    ar = row[:, None] * inv
    ac = col[:, None] * inv
    ang = jnp.concatenate([ar, ar, ac, ac], axis=-1)
    return jnp.cos(ang), jnp.sin(ang)


def apply_axial_rope(x, cos, sin):
    half = HEAD_DIM // 2
    qd = half // 2
    def rot(u):
        return jnp.concatenate([-u[..., qd:], u[..., :qd]], axis=-1)
    rotated = jnp.concatenate([rot(x[..., :half]), rot(x[..., half:])], axis=-1)
    return (x * cos + rotated * sin).astype(x.dtype)


def to_heads(t, n_heads):
    b, n, _ = t.shape
    return t.reshape(b, n, n_heads, -1).transpose(0, 2, 1, 3)


def mlstm_chunk_scan(q, k, v, ig, fg, state):
    b_, h_, n_, dh = q.shape
    nc = n_ // CHUNK
    f32 = jnp.float32
    def chunks(a):
        a = a.astype(f32)
        return jnp.moveaxis(a.reshape(b_, h_, nc, CHUNK, *a.shape[3:]), 2, 0)
    xs = (chunks(q), chunks(k), chunks(v), chunks(ig), chunks(jax.nn.log_sigmoid(fg.astype(f32))))
    causal = jnp.tril(jnp.ones((CHUNK, CHUNK), dtype=bool))
    def step(carry, inp):
        C, n, m = carry
        qb, kb, vb, ib, lfb = inp
        cum = jnp.cumsum(lfb, axis=-1)
        logw = jnp.where(causal, cum[..., :, None] - cum[..., None, :] + ib[..., None, :], -jnp.inf)
        inter = cum + m[..., None]
        m_t = jnp.maximum(inter, jnp.max(logw, axis=-1))
        w_inter = jnp.exp(inter - m_t)
        s = jnp.einsum('bhld,bhsd->bhls', qb, kb) * jnp.exp(logw - m_t[..., None])
        num = w_inter[..., None] * jnp.einsum('bhde,bhle->bhld', C, qb) + jnp.einsum('bhls,bhsd->bhld', s, vb)
        den = w_inter * jnp.einsum('bhe,bhle->bhl', n, qb) + jnp.sum(s, axis=-1)
        h = num / jnp.maximum(jnp.abs(den), jnp.exp(-m_t))[..., None]
        m_new = m_t[..., -1]
        w_old = jnp.exp(cum[..., -1] + m - m_new)
        w_src = jnp.exp(cum[..., -1:] - cum + ib - m_new[..., None])
        C_new = w_old[..., None, None] * C + jnp.einsum('bhs,bhsd,bhse->bhde', w_src, vb, kb)
        n_new = w_old[..., None] * n + jnp.einsum('bhs,bhse->bhe', w_src, kb)
        return (C_new, n_new, m_new), h
    state, hs = lax.scan(step, state, xs)
    return jnp.moveaxis(hs, 0, 2).reshape(b_, h_, n_, dh), state


def mlstm_dir(q, k, v, ig, fg, state, reverse):
    if reverse:
        q, k, v, ig, fg = (jnp.flip(t, 2) for t in (q, k, v, ig, fg))
    h, st = mlstm_chunk_scan(q, k, v, ig, fg, state)
    if reverse:
        h = jnp.flip(h, 2)
    return h, st


def mlstm_mixer(u_lat, u_ctx, gate_b, norm_g, want_ctx):
    def prep(u, rope):
        b_, n_, _ = u.shape
        q, k, v, o, g = jnp.split(u, [M_WIDTH, 2 * M_WIDTH, 3 * M_WIDTH, 4 * M_WIDTH], axis=-1)
        q, k, v = to_heads(q, M_HEADS), to_heads(k, M_HEADS) * (HEAD_DIM ** -0.5), to_heads(v, M_HEADS)
        if rope:
            cos, sin = axial_rope(n_)
            q, k = apply_axial_rope(q, cos, sin), apply_axial_rope(k, cos, sin)
        g = (g + gate_b.reshape(-1)).reshape(b_, n_, 4, M_HEADS).transpose(2, 0, 3, 1)
        return q, k, v, o, g
    ql, kl, vl, ol, gl = prep(u_lat, True)
    qc, kc, vc, oc, gc = prep(u_ctx, False)
    b_ = u_lat.shape[0]
    zero = (jnp.zeros((b_, M_HEADS, HEAD_DIM, HEAD_DIM), jnp.float32),
            jnp.zeros((b_, M_HEADS, HEAD_DIM), jnp.float32),
            jnp.zeros((b_, M_HEADS), jnp.float32))
    hc_f, st_f = mlstm_dir(qc, kc, vc, gc[0], gc[1], zero, False)
    hc_b, st_b = mlstm_dir(qc, kc, vc, gc[2], gc[3], zero, True)
    hl_f, _ = mlstm_dir(ql, kl, vl, gl[0], gl[1], st_f, False)
    hl_b, _ = mlstm_dir(ql, kl, vl, gl[2], gl[3], st_b, True)
    g_head = norm_g.reshape(M_HEADS, 1, HEAD_DIM)
    def post(h, o):
        b2, _, n2, _ = h.shape
        hn = rmsnorm(h, g_head).transpose(0, 2, 1, 3).reshape(b2, n2, M_WIDTH)
        return (jax.nn.sigmoid(o.astype(jnp.float32)) * hn).astype(o.dtype)
    y_lat = post(hl_f + hl_b, ol)
    y_ctx = post(hc_f + hc_b, oc) if want_ctx else None
    return y_lat, y_ctx


def na_mixer(u_lat, u_ctx, qn_g, kn_g, rpb, want_ctx):
    b_, n_, _ = u_lat.shape
    rows = n_ // GRID_W
    wr = min(WIN_R_MAX, rows)
    wc = min(WIN_C, GRID_W)
    scale = HEAD_DIM ** -0.5
    def qkv(u):
        q, k, v = jnp.split(u, 3, axis=-1)
        sh = u.shape[:2] + (NA_HEADS, HEAD_DIM)
        return rmsnorm(q.reshape(sh), qn_g), rmsnorm(k.reshape(sh), kn_g), v.reshape(sh)
    q, k, v = qkv(u_lat)
    qc, kc, vc = qkv(u_ctx)
    grid = (b_, rows, GRID_W, NA_HEADS, HEAD_DIM)
    qg, kg, vg = q.reshape(grid), k.reshape(grid), v.reshape(grid)
    cols = np.arange(GRID_W)
    col0 = np.clip(cols - wc // 2, 0, GRID_W - wc)
    col_idx = col0[:, None] + np.arange(wc)[None, :]
    dc = col_idx - cols[:, None] + (WIN_C - 1)
    def row_block(r):
        sr = jnp.clip(r - wr // 2, 0, rows - wr)
        q_r = lax.dynamic_index_in_dim(qg, r, axis=1, keepdims=False)
        k_band = lax.dynamic_slice_in_dim(kg, sr, wr, axis=1)
        v_band = lax.dynamic_slice_in_dim(vg, sr, wr, axis=1)
        k_win = k_band[:, :, col_idx]
        v_win = v_band[:, :, col_idx]
        s_loc = jnp.einsum('bqhd,brqchd->bhqrc', q_r, k_win).astype(jnp.float32) * scale
        dr = sr + jnp.arange(wr) - r + (WIN_R_MAX - 1)
        bias = jnp.transpose(rpb[:, dr][:, :, dc], (0, 2, 1, 3))
        s_loc = (s_loc + bias.astype(jnp.float32)).reshape(b_, NA_HEADS, GRID_W, wr * wc)
        s_ctx = jnp.einsum('bqhd,bkhd->bhqk', q_r, kc).astype(jnp.float32) * scale
        p = jax.nn.softmax(jnp.concatenate([s_loc, s_ctx], axis=-1), axis=-1).astype(v.dtype)
        p_loc = p[..., :wr * wc].reshape(b_, NA_HEADS, GRID_W, wr, wc)
        p_ctx = p[..., wr * wc:]
        o = jnp.einsum('bhqrc,brqchd->bqhd', p_loc, v_win) + jnp.einsum('bhqk,bkhd->bqhd', p_ctx, vc)
        return o.astype(v.dtype)
    out = lax.map(row_block, jnp.arange(rows))
    y_lat = jnp.transpose(out, (1, 0, 2, 3, 4)).reshape(b_, n_, NA_WIDTH)
    y_ctx = None
    if want_ctx:
        s = jnp.einsum('bqhd,bkhd->bhqk', qc, kc).astype(jnp.float32) * scale
        p = jax.nn.softmax(s, axis=-1).astype(vc.dtype)
        y_ctx = jnp.einsum('bhqk,bkhd->bqhd', p, vc).reshape(b_, u_ctx.shape[1], NA_WIDTH)
    return y_lat, y_ctx


def conv_centred(x, w, b):
    n_ = x.shape[1]
    xp = jnp.pad(x, ((0, 0), (CONV_LEFT, CONV_W - 1 - CONV_LEFT), (0, 0)))
    return sum(xp[:, j:j + n_] * w[j] for j in range(CONV_W)) + b


def block_diag(x, w, b):
    xb = x.reshape(*x.shape[:-1], LRU_BLOCKS, LRU_BLOCK)
    return jnp.einsum('bnkc,kcd->bnkd', xb, w.astype(jnp.float32)).reshape(x.shape) + b.astype(jnp.float32)


def lin_combine(left, right):
    a1, b1 = left
    a2, b2 = right
    return a1 * a2, a2 * b1 + b2


def rglru_dir(x, w_a, b_a, w_x, b_x, lam, h0, reverse):
    if reverse:
        x = jnp.flip(x, 1)
    xf = x.astype(jnp.float32)
    r = jax.nn.sigmoid(block_diag(xf, w_a, b_a))
    i = jax.nn.sigmoid(block_diag(xf, w_x, b_x))
    log_a = -LRU_C * r * jax.nn.softplus(-lam.astype(jnp.float32))
    a = jnp.exp(log_a)
    bt = jnp.sqrt(-jnp.expm1(2.0 * log_a)) * (i * xf)
    bt = bt.at[:, 0].add(a[:, 0] * h0)
    _, h = lax.associative_scan(lin_combine, (a, bt), axis=1)
    h_last = h[:, -1]
    if reverse:
        h = jnp.flip(h, 1)
    return h, h_last


def lru_mixer(u_lat, u_ctx, conv_w, conv_b, w_a, b_a, w_x, b_x, lam, want_ctx):
    xr_l, gt_l = jnp.split(u_lat, 2, axis=-1)
    xr_c, gt_c = jnp.split(u_ctx, 2, axis=-1)
    xl = conv_centred(xr_l, conv_w, conv_b)
    xc = conv_centred(xr_c, conv_w, conv_b)
    h0 = jnp.zeros((u_lat.shape[0], LRU_WIDTH), jnp.float32)
    hc_f, st_f = rglru_dir(xc, w_a[0], b_a[0], w_x[0], b_x[0], lam[0], h0, False)
    hc_b, st_b = rglru_dir(xc, w_a[1], b_a[1], w_x[1], b_x[1], lam[1], h0, True)
    hl_f, _ = rglru_dir(xl, w_a[0], b_a[0], w_x[0], b_x[0], lam[0], st_f, False)
    hl_b, _ = rglru_dir(xl, w_a[1], b_a[1], w_x[1], b_x[1], lam[1], st_b, True)
    y_lat = ((hl_f + hl_b) * jax.nn.gelu(gt_l.astype(jnp.float32))).astype(u_lat.dtype)
    y_ctx = ((hc_f + hc_b) * jax.nn.gelu(gt_c.astype(jnp.float32))).astype(u_ctx.dtype) if want_ctx else None
    return y_lat, y_ctx


def sq_relu_mlp(h, w1, w2):
    return jnp.square(jax.nn.relu(h @ w1)) @ w2


def hybrid_layer(x, xc, c_act, cctx_act, w_mod, b_mod, n1, n2, w_in, m_gate_b, m_norm_g,
                 na_qn, na_kn, na_rpb, conv_w, conv_b, l_wa, l_ba, l_wx, l_bx, l_lam,
                 w_out, w1, w2, want_ctx):
    sh1, sc1, g1, sh2, sc2, g2 = jnp.split(c_act @ w_mod + b_mod, 6, axis=-1)
    csh1, csc1, cg1, csh2, csc2, cg2 = jnp.split(cctx_act @ w_mod + b_mod, 6, axis=-1)
    u = modulate(rmsnorm(x, n1), sh1, sc1) @ w_in
    uc = modulate(rmsnorm(xc, n1), csh1, csc1) @ w_in
    sl_m, sl_na = slice(0, M_COLS), slice(M_COLS, M_COLS + NA_COLS)
    sl_l = slice(M_COLS + NA_COLS, IN_COLS)
    ym, ymc = mlstm_mixer(u[..., sl_m], uc[..., sl_m], m_gate_b, m_norm_g, want_ctx)
    yn, ync = na_mixer(u[..., sl_na], uc[..., sl_na], na_qn, na_kn, na_rpb, want_ctx)
    yl, ylc = lru_mixer(u[..., sl_l], uc[..., sl_l], conv_w, conv_b, l_wa, l_ba, l_wx, l_bx, l_lam, want_ctx)
    x = x + g1[:, None] * (jnp.concatenate([ym, yn, yl], axis=-1) @ w_out)
    x = x + g2[:, None] * sq_relu_mlp(modulate(rmsnorm(x, n2), sh2, sc2), w1, w2)
    if want_ctx:
        xc = xc + cg1[:, None] * (jnp.concatenate([ymc, ync, ylc], axis=-1) @ w_out)
        xc = xc + cg2[:, None] * sq_relu_mlp(modulate(rmsnorm(xc, n2), csh2, csc2), w1, w2)
    return x, xc


def setup_inputs(seed: int = 0) -> dict:
    key = jax.random.key(seed)
    ks = jax.random.split(key, 24)
    f32 = jnp.float32
    def nrm(k, shape, fan_in, gain=1.0):
        return jax.random.normal(k, shape, f32) * (gain * fan_in ** -0.5)
    def small(k, shape, s=0.02):
        return jax.random.normal(k, shape, f32) * s
    f_bias_base = jnp.array([0.0, 1.0, 0.0, 1.0], f32)[:, None] * jnp.linspace(3.0, 6.0, M_HEADS, dtype=f32)[None, :]
    a_target = jax.random.uniform(ks[20], (DEPTH, 2, LRU_WIDTH), f32, 0.9, 0.999)
    sig_l = a_target ** (1.0 / LRU_C)
    lam = jnp.log(sig_l) - jnp.log1p(-sig_l)
    return {
        'x': jax.random.normal(ks[0], (BATCH, SEQ, D_MODEL), f32),
        'c': jax.random.normal(ks[1], (BATCH, D_MODEL), f32),
        'ctx': jax.random.normal(ks[2], (BATCH, CTX_LEN, D_MODEL), f32),
        'c_ctx': jax.random.normal(ks[3], (D_MODEL,), f32),
        'w_mod': nrm(ks[4], (DEPTH, D_MODEL, 6 * D_MODEL), D_MODEL, 0.5),
        'b_mod': small(ks[5], (DEPTH, 6 * D_MODEL)),
        'norm1_g': 1.0 + small(ks[6], (DEPTH, D_MODEL)),
        'norm2_g': 1.0 + small(ks[7], (DEPTH, D_MODEL)),
        'w_in': nrm(ks[8], (DEPTH, D_MODEL, IN_COLS), D_MODEL),
        'mlstm_gate_b': f_bias_base[None] + small(ks[9], (DEPTH, 4, M_HEADS), 0.1),
        'mlstm_norm_g': 1.0 + small(ks[10], (DEPTH, M_WIDTH)),
        'na_q_norm_g': 1.0 + small(ks[11], (DEPTH, HEAD_DIM)),
        'na_k_norm_g': 1.0 + small(ks[12], (DEPTH, HEAD_DIM)),
        'na_rpb': small(ks[13], (DEPTH, NA_HEADS, RPB_R, RPB_C), 0.1),
        'lru_conv_w': nrm(ks[14], (DEPTH, CONV_W, LRU_WIDTH), CONV_W),
        'lru_conv_b': small(ks[15], (DEPTH, LRU_WIDTH)),
        'lru_w_a': nrm(ks[16], (DEPTH, 2, LRU_BLOCKS, LRU_BLOCK, LRU_BLOCK), LRU_BLOCK),
        'lru_b_a': small(ks[17], (DEPTH, 2, LRU_WIDTH)),
        'lru_w_x': nrm(ks[18], (DEPTH, 2, LRU_BLOCKS, LRU_BLOCK, LRU_BLOCK), LRU_BLOCK),
        'lru_b_x': small(ks[19], (DEPTH, 2, LRU_WIDTH)),
        'lru_lambda': lam,
        'w_out': nrm(ks[21], (DEPTH, D_MODEL, D_MODEL), D_MODEL),
        'w_mlp1': nrm(ks[22], (DEPTH, D_MODEL, MLP_HIDDEN), D_MODEL),
        'w_mlp2': nrm(ks[23], (DEPTH, MLP_HIDDEN, D_MODEL), MLP_HIDDEN),
    }


def reference(x, c, ctx, c_ctx, w_mod, b_mod, norm1_g, norm2_g, w_in, mlstm_gate_b, mlstm_norm_g,
              na_q_norm_g, na_k_norm_g, na_rpb, lru_conv_w, lru_conv_b, lru_w_a, lru_b_a, lru_w_x,
              lru_b_x, lru_lambda, w_out, w_mlp1, w_mlp2):
    c_act = jax.nn.silu(c)
    cctx_act = jax.nn.silu(c_ctx)[None]
    xc = ctx
    for l in range(DEPTH):
        x, xc = hybrid_layer(x, xc, c_act, cctx_act, w_mod[l], b_mod[l], norm1_g[l], norm2_g[l], w_in[l],
                             mlstm_gate_b[l], mlstm_norm_g[l], na_q_norm_g[l], na_k_norm_g[l], na_rpb[l],
                             lru_conv_w[l], lru_conv_b[l], lru_w_a[l], lru_b_a[l], lru_w_x[l], lru_b_x[l],
                             lru_lambda[l], w_out[l], w_mlp1[l], w_mlp2[l], l < DEPTH - 1)
    return x
```

```python
import contextlib
import numpy as np
import concourse.bass as bass
import concourse.mybir as mybir
from concourse.bass_utils import run_bass_kernel_spmd

F32 = mybir.dt.float32
BF16 = mybir.dt.bfloat16
AF = mybir.ActivationFunctionType
ALU = mybir.AluOpType
AX = mybir.AxisListType

D = 1024
NCTX = 256
NLAT = 4096
NTOK = NCTX + NLAT
NCH = NTOK // 128
GRID_W = 64
HD = 64
EPS = 1e-6
NEG = -30000.0
MLPH = 4096
C_NQ, C_NK, C_LX, C_LG, C_MQ, C_MK, C_MV, C_MO, C_NV, C_GI, C_GF = 0, 512, 1024, 1280, 1536, 1792, 2048, 2304, 2560, 3072, 3080
NCOL = 3088


class Sched:
    ENG = ('pe', 'act', 'dve', 'pool', 'sp')

    def __init__(self, nc, stack, n_dma_sems=48):
        self.nc = nc
        self.q = {e: [] for e in self.ENG}
        self.cnt = {e: 0 for e in self.ENG}
        self.sem = {e: stack.enter_context(nc.semaphore("s_" + e)) for e in ('pe', 'act', 'dve', 'pool')}
        self.dsem = [[stack.enter_context(nc.semaphore("d%d" % i)), 0] for i in range(n_dma_sems)]
        self.dpool = {'sp': list(range(0, n_dma_sems // 2)), 'pool': list(range(n_dma_sems // 2, n_dma_sems)),
                      'act': list(range(0, n_dma_sems // 2))}
        self.drr = {'sp': 0, 'pool': 0, 'act': 0}
        self.seen = {}
        self.res = {}
        self.ninstr = 0

    def _semh(self, key):
        return self.sem[key] if isinstance(key, str) else self.dsem[key[1]][0]

    def _wait(self, eng, tok):
        key, val = tok
        if self.seen.get((eng, key), 0) >= val:
            return
        self.seen[(eng, key)] = val
        h = self._semh(key)
        self.q[eng].append(lambda e, h=h, val=val: e.wait_ge(h, val))

    def _deps(self, eng, reads, writes):
        toks = []
        for r in reads:
            st = self.res.get(r)
            if st and st[0]:
                toks.append((st[0], 'raw'))
            if st and isinstance(r, str) and r.startswith('ps'):
                for t in st[1]:
                    if t[0] != eng:
                        toks.append((t, 'rar'))
        for w in writes:
            st = self.res.get(w)
            if st:
                if st[0]:
                    toks.append((st[0], 'waw'))
                for t in st[1]:
                    toks.append((t, 'war'))
        for tok, kind in toks:
            key, val = tok
            if key == eng:
                if eng == 'pe':
                    continue
                if eng != 'pool' and self.cnt[eng] - val >= 8:
                    continue
            self._wait(eng, tok)

    def _commit(self, tok, reads, writes):
        for r in reads:
            st = self.res.setdefault(r, [None, []])
            st[1].append(tok)
            if len(st[1]) > 48:
                best = {}
                for k, v in st[1]:
                    if best.get(k, 0) < v:
                        best[k] = v
                st[1] = list(best.items())
        for w in writes:
            self.res[w] = [tok, []]

    def op(self, eng, fn, reads=(), writes=()):
        self._deps(eng, reads, writes)
        self.cnt[eng] += 1
        h = self.sem[eng]
        self.q[eng].append(lambda e, fn=fn, h=h: fn(e).then_inc(h, 1))
        tok = (eng, self.cnt[eng])
        self._commit(tok, reads, writes)
        self.ninstr += 1
        return tok

    def op_nosig(self, eng, fn, reads=(), writes=()):
        self._deps(eng, reads, writes)
        self.q[eng].append(lambda e, fn=fn: fn(e))
        self.ninstr += 1

    def dma(self, eng, out, in_, reads=(), writes=(), **kw):
        self._deps(eng, reads, writes)
        pool_ = self.dpool[eng]
        idx = pool_[self.drr[eng] % len(pool_)]
        self.drr[eng] += 1
        ent = self.dsem[idx]
        key = ('dma', idx)
        if ent[1] > 0:
            self._wait(eng, (key, ent[1]))
        ent[1] += 16
        h = ent[0]
        self.q[eng].append(lambda e, h=h, out=out, in_=in_, kw=kw: e.dma_start(out=out, in_=in_, **kw).then_inc(h, 16))
        tok = (key, ent[1])
        self._commit(tok, reads, writes)
        self.ninstr += 1
        return tok

    def barrier(self):
        for e in self.ENG:
            for i, ent in enumerate(self.dsem):
                if ent[1] > 0:
                    self._wait(e, (('dma', i), ent[1]))
            for o in ('pe', 'act', 'dve', 'pool'):
                if o != e and self.cnt[o] > 0:
                    self._wait(e, (o, self.cnt[o]))
        self.res = {}

    def flush(self):
        self.barrier()
        nc = self.nc
        q = self.q
        with nc.Block() as block:
            @block.tensor
            def _(e):
                for f in q['pe']:
                    f(e)

            @block.scalar
            def _(e):
                for f in q['act']:
                    f(e)

            @block.vector
            def _(e):
                for f in q['dve']:
                    f(e)

            @block.gpsimd
            def _(e):
                for f in q['pool']:
                    f(e)

            @block.sync
            def _(e):
                for f in q['sp']:
                    f(e)
        self.q = {e: [] for e in self.ENG}


def na_plan():
    rows, wr = 64, 8

    def sr(r):
        return min(max(r - 4, 0), rows - wr)
    plans = []
    for j in range(32):
        kts = sorted(set((sr(2 * j + a) + i) // 2 for a in (0, 1) for i in range(8)))
        entry = []
        for kt in kts:
            blocks = []
            for b in (0, 1):
                for a in (0, 1):
                    r, rp = 2 * j + a, 2 * kt + b
                    s0 = sr(r)
                    blocks.append(rp - r + 7 if s0 <= rp < s0 + 8 else None)
            entry.append((kt - j, tuple(blocks)))
        plans.append((kts, tuple(entry)))
    cases = {}
    off = 0
    for kts, sig in plans:
        if sig not in cases:
            cases[sig] = off
            off += len(sig)
    return plans, cases, off


def _bf_layout_small(a, L):
    return np.ascontiguousarray(a)


def prep_inputs(inp, depth):
    L = depth
    f32 = np.float32
    g = {k: np.asarray(v, dtype=f32) for k, v in inp.items()}
    B = g['x'].shape[0]
    m0, na0, l0 = 0, 1040, 2576
    gi = [1024 + i for i in (0, 1, 2, 3, 8, 9, 10, 11)]
    gf = [1024 + i for i in (4, 5, 6, 7, 12, 13, 14, 15)]
    cols = (list(range(na0, na0 + 512)) + list(range(na0 + 512, na0 + 1024)) +
            list(range(l0, l0 + 256)) + list(range(l0 + 256, l0 + 512)) +
            list(range(0, 256)) + list(range(256, 512)) + list(range(512, 768)) + list(range(768, 1024)) +
            list(range(na0 + 1024, na0 + 1536)) + gi + gf)
    assert len(cols) == NCOL
    shared = {}
    shared['w_in'] = np.ascontiguousarray(g['w_in'][:L][:, :, cols])
    shared['w_mod'] = np.ascontiguousarray(g['w_mod'][:L])
    shared['w_out'] = np.ascontiguousarray(g['w_out'][:L])
    shared['w1'] = np.ascontiguousarray(g['w_mlp1'][:L])
    shared['w2'] = np.ascontiguousarray(g['w_mlp2'][:L])
    shared['bmodT'] = np.ascontiguousarray(g['b_mod'][:L].reshape(L, 48, 128).transpose(2, 0, 1))
    shared['n1T'] = np.ascontiguousarray(g['norm1_g'][:L].reshape(L, 8, 128).transpose(2, 0, 1))
    shared['n2T'] = np.ascontiguousarray(g['norm2_g'][:L].reshape(L, 8, 128).transpose(2, 0, 1))
    gb = g['mlstm_gate_b'][:L]
    gtok = np.concatenate([gb[:, 0], gb[:, 2], gb[:, 1], gb[:, 3]], axis=1)
    shared['gateb'] = np.ascontiguousarray(np.broadcast_to(gtok[None], (128, L, 16)))
    shared['mng'] = np.ascontiguousarray(np.broadcast_to(g['mlstm_norm_g'][:L][None], (128, L, 256)))
    shared['nqg'] = np.ascontiguousarray(np.tile(g['na_q_norm_g'][:L], (1, 2)).T)
    shared['nkg'] = np.ascontiguousarray(np.tile(g['na_k_norm_g'][:L], (1, 2)).T)
    cc = np.arange(64)
    col0 = np.clip(cc - 8, 0, 48)
    cp = np.arange(64)
    valid = (cp[:, None] >= col0[None, :]) & (cp[:, None] < col0[None, :] + 16)
    didx = np.clip(cp[:, None] - cc[None, :] + 15, 0, 30)
    rpb = g['na_rpb'][:L]
    tb = rpb[:, :, :, didx]
    tb = np.where(valid[None, None, None], tb, f32(NEG))
    shared['TB'] = np.ascontiguousarray(tb.transpose(0, 3, 2, 1, 4)).astype(f32)
    shared['convw'] = np.ascontiguousarray(g['lru_conv_w'][:L].reshape(L, 4, 2, 128).transpose(3, 0, 2, 1))
    shared['convb'] = np.ascontiguousarray(g['lru_conv_b'][:L].reshape(L, 2, 128).transpose(2, 0, 1))
    for nm, key in (('lba', 'lru_b_a'), ('lbx', 'lru_b_x'), ('llam', 'lru_lambda')):
        shared[nm] = np.ascontiguousarray(g[key][:L].reshape(L, 2, 2, 128).transpose(3, 0, 1, 2))
    shared['lwa'] = np.ascontiguousarray(g['lru_w_a'][:L])
    shared['lwx'] = np.ascontiguousarray(g['lru_w_x'][:L])
    t = np.arange(NLAT)
    row = (t // GRID_W).astype(f32)
    col = (t % GRID_W).astype(f32)
    inv = (f32(10000.0) ** (-np.arange(16, dtype=f32) / f32(16))).astype(f32)
    ar = row[:, None] * inv
    ac = col[:, None] * inv
    ang = np.concatenate([ar, ar, ac, ac], axis=-1)
    sgn = np.concatenate([-np.ones(16), np.ones(16), -np.ones(16), np.ones(16)]).astype(f32)
    cos = np.cos(ang).astype(f32)
    sin = (np.sin(ang) * sgn).astype(f32)
    shared['ropec'] = np.ascontiguousarray(cos.reshape(32, 128, 64).transpose(1, 0, 2))
    shared['ropes'] = np.ascontiguousarray(sin.reshape(32, 128, 64).transpose(1, 0, 2))
    i128 = np.arange(128)
    consts = np.zeros((128, 6, 128), f32)
    consts[:, 0, :] = np.eye(128)
    consts[:, 1, :] = (i128[:, None] <= i128[None, :])
    consts[:, 2, :] = (i128[:, None] >= i128[None, :])
    consts[127, 3, :] = 1.0
    consts[0, 4, :] = 1.0
    consts[:, 5, :] = (i128[:, None] // 64 == i128[None, :] // 64)
    shared['consts'] = consts
    maps = []
    for b in range(B):
        m = dict(shared)
        m['xT0'] = np.ascontiguousarray(np.concatenate([g['ctx'][b].T, g['x'][b].T], axis=1))
        cT = np.stack([g['c'][b].reshape(8, 128).T, g['c_ctx'].reshape(8, 128).T], axis=-1)
        m['cT'] = np.ascontiguousarray(cT)
        maps.append(m)
    return maps


def bc(ap, pos, n):
    dims = [list(d) for d in ap.ap]
    dims.insert(1 + pos, [0, n])
    return bass.AP(tensor=ap.tensor, offset=ap.offset, ap=dims)


class Ctx:
    pass


def build(depth, dbg=(), stop_after=None):
    nc = bass.Bass("TRN2", target_bir_lowering=False)
    L = depth
    plans, cases, nslots = na_plan()

    def din(name, shape):
        return nc.dram_tensor(name, list(shape), F32, kind="ExternalInput").ap()

    def dscr(name, shape, dt):
        return nc.dram_tensor(name, list(shape), dt, kind=("ExternalOutput" if name in dbg else "Internal")).ap()

    xT0 = din('xT0', [D, NTOK]); cT = din('cT', [128, 8, 2])
    w_in = din('w_in', [L, D, NCOL]); w_mod = din('w_mod', [L, D, 6 * D]); w_out = din('w_out', [L, D, D])
    w1 = din('w1', [L, D, MLPH]); w2 = din('w2', [L, MLPH, D])
    bmodT_d = din('bmodT', [128, L, 48]); n1T_d = din('n1T', [128, L, 8]); n2T_d = din('n2T', [128, L, 8])
    gateb_d = din('gateb', [128, L, 16]); mng_d = din('mng', [128, L, 256])
    nqg_d = din('nqg', [128, L]); nkg_d = din('nkg', [128, L])
    TB = din('TB', [L, 64, 15, 8, 64])
    convw_d = din('convw', [128, L, 2, 4]); convb_d = din('convb', [128, L, 2])
    lba_d = din('lba', [128, L, 2, 2]); lbx_d = din('lbx', [128, L, 2, 2]); llam_d = din('llam', [128, L, 2, 2])
    lwa = din('lwa', [L, 2, 4, 64, 64]); lwx = din('lwx', [L, 2, 4, 64, 64])
    ropec_d = din('ropec', [128, 32, 64]); ropes_d = din('ropes', [128, 32, 64])
    consts_d = din('consts', [128, 6, 128])
    outT = nc.dram_tensor('outT', [D, NLAT], F32, kind="ExternalOutput").ap()

    xT = dscr('xT', [D, NTOK], F32)
    wb_in = dscr('wb_in', [L, D, NCOL], BF16); wb_out = dscr('wb_out', [L, D, D], BF16)
    wb_1 = dscr('wb_1', [L, D, MLPH], BF16); wb_2 = dscr('wb_2', [L, MLPH, D], BF16)
    m_q = dscr('m_q', [NTOK, 256], BF16); m_k = dscr('m_k', [NTOK, 256], BF16)
    m_v1 = dscr('m_v1', [NTOK, 260], BF16); m_o = dscr('m_o', [NTOK, 256], BF16); m_g = dscr('m_g', [NTOK, 16], F32)
    n_qT = dscr('n_qT', [512, NTOK], BF16); n_kT = dscr('n_kT', [512, NTOK], BF16); n_v1 = dscr('n_v1', [NTOK, 520], BF16)
    l_xT = dscr('l_xT', [256, NTOK], BF16); l_gT = dscr('l_gT', [256, NTOK], BF16)
    mixT = dscr('mixT', [D, NTOK], BF16)

    with contextlib.ExitStack() as st:
        S = Sched(nc, st)

        uid = [0]

        def sb(stack, name, shape, dt):
            uid[0] += 1
            return stack.enter_context(nc.sbuf_tensor("sb%d_%s" % (uid[0], name), list(shape), dt))

        NPS = 7
        PS = [st.enter_context(nc.psum_tensor("ps%d" % i, [128, 512], F32)) for i in range(NPS)]
        PST = st.enter_context(nc.psum_tensor("pst", [128, 1024], BF16))
        psi = [0]

        def next_ps():
            i = psi[0]
            psi[0] = (i + 1) % NPS
            return PS[i], "ps%d" % i

        def mm(out, pairs, wname, rnames):
            n = len(pairs)
            for i, (l_, r_) in enumerate(pairs):
                f_ = (lambda e, l_=l_, r_=r_, i=i: e.matmul(out, lhsT=l_, rhs=r_, start=(i == 0), stop=(i == n - 1)))
                if i == n - 1:
                    S.op('pe', f_, reads=rnames, writes=[wname])
                else:
                    S.op_nosig('pe', f_, reads=rnames, writes=[wname])

        def act(out, in_, func, reads, writes, **kw):
            return S.op('act', lambda e: e.activation(out=out, in_=in_, func=func, **kw), reads=reads, writes=writes)

        def tt(out, in0, in1, op, reads, writes, eng='dve'):
            return S.op(eng, lambda e: e.tensor_tensor(out=out, in0=in0, in1=in1, op=op), reads=reads, writes=writes)

        def ts(out, in0, s1, op0, reads, writes, s2=None, op1=None, eng='dve'):
            if op1 is None:
                return S.op(eng, lambda e: e.tensor_scalar(out=out, in0=in0, scalar1=s1, scalar2=None, op0=op0), reads=reads, writes=writes)
            return S.op(eng, lambda e: e.tensor_scalar(out=out, in0=in0, scalar1=s1, scalar2=s2, op0=op0, op1=op1), reads=reads, writes=writes)

        def stt(out, in0, scalar, in1, op0, op1, reads, writes):
            return S.op('dve', lambda e: e.scalar_tensor_tensor(out=out, in0=in0, scalar=scalar, in1=in1, op0=op0, op1=op1),
                        reads=reads, writes=writes)

        def cp(out, in_, reads, writes, eng='dve'):
            return S.op(eng, lambda e: e.tensor_copy(out=out, in_=in_), reads=reads, writes=writes)

        constf = sb(st, 'constf', [128, 6, 128], F32)
        cb = sb(st, 'cb', [128, 6, 128], BF16)
        onesb = sb(st, 'onesb', [128, 128], BF16)
        c_one = sb(st, 'c_one', [128, 1], F32)
        c_eps = sb(st, 'c_eps', [128, 1], F32)
        modT = sb(st, 'modT', [128, L, 48, 2], F32)
        A1v = sb(st, 'A1v', [128, L, 2, 8], F32)
        A2v = sb(st, 'A2v', [128, L, 2, 8], F32)
        n1T = sb(st, 'n1T', [128, L, 8], F32); n2T = sb(st, 'n2T', [128, L, 8], F32)
        bmodT = sb(st, 'bmodT', [128, L, 48], F32)
        gateb = sb(st, 'gateb', [128, L, 16], F32)
        nqg = sb(st, 'nqg', [128, L], F32); nkg = sb(st, 'nkg', [128, L], F32)
        convw = sb(st, 'convw', [128, L, 2, 4], F32); convb = sb(st, 'convb', [128, L, 2], F32)
        lba = sb(st, 'lba', [128, L, 2, 2], F32); lbx = sb(st, 'lbx', [128, L, 2, 2], F32)
        llam = sb(st, 'llam', [128, L, 2, 2], F32)
        cvec = sb(st, 'cvec', [128, L, 2, 2], F32); cvec2 = sb(st, 'cvec2', [128, L, 2, 2], F32)
        cact = sb(st, 'cact', [128, 8, 2], F32)
        identf = constf[:, 0, :]
        identb = cb[:, 0, :]
        blockones = cb[:, 5, :]

        if stop_after is not None:
            touch = sb(st, 'touch', [1, 64], F32)
            S.op('dve', lambda e: e.memset(touch[:], 0.0), writes=['touch'])
            for ti, t_ in enumerate((xT0, w_in, w_mod, w_out, w1, w2, TB, lwa, lwx, ropec_d, ropes_d)):
                idx = tuple([slice(0, 1)] * (len(t_.shape) - 1) + [slice(0, 2)])
                src_ = t_[idx]
                while len(src_.shape) > 2:
                    src_ = src_.squeeze(0)
                S.dma('sp', touch[0:1, 2 * ti:2 * ti + 2], src_, writes=['touch'])
            S.dma('sp', outT[0:1, 0:64], touch[0:1, :], reads=['touch'])
        S.dma('sp', constf[:], consts_d[:, :, :], writes=['constf'])
        for tl, src, nm in ((n1T, n1T_d, 'n1T'), (n2T, n2T_d, 'n2T'), (bmodT, bmodT_d, 'bmodT'), (gateb, gateb_d, 'gateb'),
                            (convw, convw_d, 'convw'), (convb, convb_d, 'convb'),
                            (lba, lba_d, 'lba'), (lbx, lbx_d, 'lbx'), (llam, llam_d, 'llam'), (cact, cT, 'cact'),
                            (nqg, nqg_d, 'nqg'), (nkg, nkg_d, 'nkg')):
            S.dma('sp', tl[:], src, writes=[nm])
        import os
        DBGV = int(os.environ.get('DBGV', '9'))
        if DBGV >= 1:
            cp(cb[:], constf[:], ['constf'], ['cb'])
        if DBGV >= 2:
            S.op('dve', lambda e: e.memset(onesb[:], 1.0), writes=['onesb'])
            S.op('dve', lambda e: e.memset(c_one[:], 1.0), writes=['c_one'])
            S.op('dve', lambda e: e.memset(c_eps[:], EPS), writes=['c_eps'])
        if DBGV >= 3:
            act(cact[:], cact[:], AF.Silu, ['cact'], ['cact'])
        if DBGV >= 4:
            ts(nqg[:], nqg[:], 0.125, ALU.mult, ['nqg'], ['nqg'])
        if DBGV >= 5:
            act(cvec[:], llam[:], AF.Exp, ['llam'], ['cvec'], scale=-1.0)
        if DBGV >= 6:
            act(cvec[:], cvec[:], AF.Ln, ['cvec', 'c_one'], ['cvec'], bias=c_one[:, 0:1])
        if DBGV >= 7:
            ts(cvec2[:], cvec[:], -16.0, ALU.mult, ['cvec'], ['cvec2'])
            ts(cvec[:], cvec[:], -8.0, ALU.mult, ['cvec', 'cvec2'], ['cvec'])

        if stop_after == 'P0':
            S.flush()
            return nc

        def cast_weights(l):
            kw = dict(max_dma_last_dim=2048)
            for k in range(8):
                S.dma('pool', wb_in[l, k * 128:(k + 1) * 128, :], w_in[l, k * 128:(k + 1) * 128, :], **kw)
            for k in range(8):
                S.dma('pool', wb_out[l, k * 128:(k + 1) * 128, :], w_out[l, k * 128:(k + 1) * 128, :], **kw)
            for k in range(8):
                S.dma('pool', wb_1[l, k * 128:(k + 1) * 128, :], w1[l, k * 128:(k + 1) * 128, :], **kw)
            for k in range(32):
                S.dma('pool', wb_2[l, k * 128:(k + 1) * 128, :], w2[l, k * 128:(k + 1) * 128, :], **kw)

        if stop_after != 'P2only':
            cast_weights(0)
        if stop_after == 'P1':
            S.flush()
            return nc

        def mod_slab(l, i, wm, it):
            w_ = wm[it % 2]; wn = 'wm%d' % (it % 2)
            S.dma('sp', w_[:], w_mod[l, :, i * 512:(i + 1) * 512].rearrange("(k p) n -> p k n", p=128), writes=[wn])
            ps, pn = next_ps()
            for jj in range(4):
                mm(ps[:, jj * 2:jj * 2 + 2], [(w_[:, k, jj * 128:(jj + 1) * 128], cact[:, k, :]) for k in range(8)],
                   pn, [wn, 'cact'])
            for j in range(2):
                tt(modT[:, l, i * 4:(i + 1) * 4, j], ps[:, j:8:2], bmodT[:, l, i * 4:(i + 1) * 4], ALU.add,
                   [pn, 'bmodT'], ['modT%d' % l])

        def mod_finish(l):
            for j in range(2):
                stt(A1v[:, l, j, :], modT[:, l, 8:16, j], 1.0, n1T[:, l, :], ALU.add, ALU.mult, ['modT%d' % l, 'n1T'], ['A1v'])
                stt(A2v[:, l, j, :], modT[:, l, 32:40, j], 1.0, n2T[:, l, :], ALU.add, ALU.mult, ['modT%d' % l, 'n2T'], ['A2v'])

        with contextlib.ExitStack() as ph:
            wm = [sb(ph, 'wm%d' % i, [128, 8, 512], F32) for i in range(2)]
            for i in range(12):
                mod_slab(0, i, wm, i)
            mod_finish(0)
            S.flush()
        if stop_after in ('P2', 'P2only'):
            return nc
        def phase_A(l):
            xsrc = xT0 if l == 0 else xT
            with contextlib.ExitStack() as ph:
                W = sb(ph, 'wA', [128, 8, NCOL], BF16)
                xts = [sb(ph, 'xtA%d' % i, [128, 8, 512], F32) for i in range(2)]
                sq = sb(ph, 'sqA', [128, 8, 512], BF16)
                hTs = [sb(ph, 'hTA%d' % i, [128, 8, 512], BF16) for i in range(2)]
                lnv = sb(ph, 'lnvA', [128, 512], F32)
                rstd = sb(ph, 'rstdA', [128, 512], F32)
                t1 = [sb(ph, 't1A%d' % i, [128, 512], F32) for i in range(2)]
                nsq = [sb(ph, 'nsq%d' % i, [128, 512], BF16) for i in range(2)]
                nln = [sb(ph, 'nln%d' % i, [128, 512], F32) for i in range(2)]
                nrs = [sb(ph, 'nrs%d' % i, [128, 512], F32) for i in range(2)]
                mqk_st = sb(ph, 'mqk_st', [128, 4, 512], BF16)
                mv_st = sb(ph, 'mv_st', [128, 4, 4, 65], BF16)
                mo_st = sb(ph, 'mo_st', [128, 4, 256], BF16)
                mg_st = sb(ph, 'mg_st', [128, 4, 16], F32)
                nv_st = sb(ph, 'nv_st', [128, 4, 8, 65], BF16)
                nq_st = sb(ph, 'nq_st', [128, 4, 512], BF16)
                nk_st = sb(ph, 'nk_st', [128, 4, 512], BF16)
                lx_st = sb(ph, 'lx_st', [128, 2, 512], BF16)
                lg_st = sb(ph, 'lg_st', [128, 2, 512], BF16)
                slabs = [(0, 256, 1)] + [(256 + 512 * i, 512, 0) for i in range(8)]
                RCa = [sb(ph, 'aRC%d' % i, [128, 4, 64], F32) for i in range(2)]
                RSa = [sb(ph, 'aRS%d' % i, [128, 4, 64], F32) for i in range(2)]
                rt1 = sb(ph, 'art1', [128, 512], F32)
                rt2 = sb(ph, 'art2', [128, 512], F32)

                def xbuf(si):
                    return xts[si % 2], 'xtA%d' % (si % 2)

                def hbuf(si):
                    return hTs[si % 2], 'hTA%d' % (si % 2)

                def norm1(si):
                    t0, T, j = slabs[si]
                    xt, xn = xbuf(si)
                    S.dma('sp', xt[:, :, 0:T], xsrc[:, t0:t0 + T].rearrange("(k p) t -> p k t", p=128), writes=[xn])
                    if si >= 1:
                        S.dma('sp', RCa[si % 2][:], ropec_d[:, 4 * (si - 1):4 * si, :], writes=['aRC%d' % (si % 2)])
                        S.dma('sp', RSa[si % 2][:], ropes_d[:, 4 * (si - 1):4 * si, :], writes=['aRS%d' % (si % 2)])
                    act(sq[:, :, 0:T], xt[:, :, 0:T], AF.Square, [xn], ['sqA'])

                ones_ps = {}

                def norm2(si):
                    t0, T, j = slabs[si]
                    ps, pn = next_ps()
                    mm(ps[:, 0:T], [(onesb[:, :], sq[:, k, 0:T]) for k in range(8)], pn, ['sqA', 'onesb'])
                    act(lnv[:, 0:T], ps[:, 0:T], AF.Ln, [pn, 'c_eps'], ['lnvA'], scale=1.0 / D, bias=c_eps[:, 0:1])

                def norm3(si):
                    t0, T, j = slabs[si]
                    xt, xn = xbuf(si)
                    hT, hn = hbuf(si)
                    act(rstd[:, 0:T], lnv[:, 0:T], AF.Exp, ['lnvA'], ['rstdA'], scale=-0.5)
                    for k in range(8):
                        tb_ = t1[k % 2]; tn = 't1A%d' % (k % 2)
                        stt(tb_[:, 0:T], xt[:, k, 0:T], A1v[:, l, j, k:k + 1], rstd[:, 0:T], ALU.mult, ALU.mult,
                            [xn, 'rstdA', 'A1v'], [tn])
                        act(hT[:, k, 0:T], tb_[:, 0:T], AF.Identity, [tn, 'modT'], [hn], bias=modT[:, l, k, j:j + 1], scale=1.0)

                nnc = [0]

                def groups(si):
                    t0, T, j = slabs[si]
                    hT, hn = hbuf(si)
                    G = []

                    def fm(col, M=128):
                        ps_, pn_ = next_ps()
                        mm(ps_[0:M, 0:T], [(W[:, k, col:col + M], hT[:, k, 0:T]) for k in range(8)], pn_, [hn, 'wA'])
                        return ps_, pn_

                    held = {}

                    def g_nqk1(key, c0, i):
                        b_ = nnc[0] % 2; nnc[0] += 1
                        ps_, pn_ = fm(c0 + 128 * i)
                        act(nsq[b_][:, 0:T], ps_[:, 0:T], AF.Square, [pn_], ['nsq%d' % b_])
                        held[key] = (b_, ps_, pn_)

                    def g_nqk2(key, gvec, gname, stg, sname, i):
                        b_, ps_, pn_ = held.pop(key)
                        ps2, pn2 = next_ps()
                        mm(ps2[:, 0:T], [(blockones, nsq[b_][:, 0:T])], pn2, ['nsq%d' % b_, 'cb'])
                        act(nln[b_][:, 0:T], ps2[:, 0:T], AF.Ln, [pn2, 'c_eps'], ['nln%d' % b_], scale=1.0 / HD, bias=c_eps[:, 0:1])
                        act(nrs[b_][:, 0:T], nln[b_][:, 0:T], AF.Exp, ['nln%d' % b_], ['nrs%d' % b_], scale=-0.5)
                        stt(stg[:, i, 0:T], ps_[:, 0:T], gvec[:, l:l + 1], nrs[b_][:, 0:T], ALU.mult, ALU.mult,
                            [pn_, 'nrs%d' % b_, gname], [sname])
                    p1s, p2s = [], []
                    for (c0, gvec, gname, stg, sname) in ((C_NQ, nqg, 'nqg', nq_st, 'nq_st'), (C_NK, nkg, 'nkg', nk_st, 'nk_st')):
                        for i in range(4):
                            key = (c0, i)
                            p1s.append(lambda key=key, c0=c0, i=i: g_nqk1(key, c0, i))
                            p2s.append(lambda key=key, gvec=gvec, gname=gname, stg=stg, sname=sname, i=i: g_nqk2(key, gvec, gname, stg, sname, i))
                    G.append(p1s[0])
                    for gi in range(8):
                        if gi + 1 < 8:
                            G.append(p1s[gi + 1])
                        G.append(p2s[gi])

                    def g_lx(i):
                        ps_, pn_ = fm(C_LX + 128 * i)
                        act(lx_st[:, i, 0:T], ps_[:, 0:T], AF.Copy, [pn_], ['lx_st'])

                    def g_lg(i):
                        ps_, pn_ = fm(C_LG + 128 * i)
                        act(lg_st[:, i, 0:T], ps_[:, 0:T], AF.Gelu, [pn_], ['lg_st'])
                    for i in range(2):
                        G.append(lambda i=i: g_lx(i))
                    for i in range(2):
                        G.append(lambda i=i: g_lg(i))

                    def g_tok(s, kind, si=si):
                        tsl = slice(s * 128, (s + 1) * 128)
                        ps_, pn_ = next_ps()
                        if kind == 0:
                            mm(ps_[:, :], [(hT[:, k, tsl], W[:, k, C_MQ:C_MQ + 512]) for k in range(8)], pn_, [hn, 'wA'])
                            if si == 0:
                                cp(mqk_st[:, s, :], ps_[:, :], [pn_], ['mqk_st'])
                            else:
                                rc = RCa[si % 2]; rs = RSa[si % 2]
                                rcn = 'aRC%d' % (si % 2); rsn_ = 'aRS%d' % (si % 2)
                                tt(rt1[:].rearrange("p (h d) -> p h d", d=64), ps_[:, :].rearrange("p (h d) -> p h d", d=64),
                                   bc(rc[:, s, :], 0, 8), ALU.mult, [pn_, rcn], ['art1'])
                                qv = ps_[:, :].rearrange("p (h f a e) -> p h f a e", h=8, f=2, a=2, e=16)
                                tv = rt2[:].rearrange("p (h f a e) -> p h f a e", h=8, f=2, a=2, e=16)
                                rv = rs[:, s, :].rearrange("p (f a e) -> p f a e", f=2, a=2, e=16)
                                for a_ in range(2):
                                    tt(tv[:, :, :, a_, :], qv[:, :, :, 1 - a_, :], bc(rv[:, :, a_, :], 0, 8), ALU.mult, [pn_, rsn_], ['art2'])
                                tt(mqk_st[:, s, :], rt1[:], rt2[:], ALU.add, ['art1', 'art2'], ['mqk_st'])
                        elif kind == 1:
                            mm(ps_[:, :], [(hT[:, k, tsl], W[:, k, C_MV:C_MV + 512]) for k in range(8)], pn_, [hn, 'wA'])
                            cp(mv_st[:, s, :, 0:64], ps_[:, 0:256].rearrange("p (h d) -> p h d", d=64), [pn_], ['mv_st'])
                            act(mo_st[:, s, :], ps_[:, 256:512], AF.Sigmoid, [pn_], ['mo_st'])
                        elif kind == 2:
                            mm(ps_[:, :], [(hT[:, k, tsl], W[:, k, C_NV:C_NV + 512]) for k in range(8)], pn_, [hn, 'wA'])
                            cp(nv_st[:, s, :, 0:64], ps_[:, :].rearrange("p (h d) -> p h d", d=64), [pn_], ['nv_st'])
                        else:
                            mm(ps_[:, 0:16], [(hT[:, k, tsl], W[:, k, C_GI:C_GI + 16]) for k in range(8)], pn_, [hn, 'wA'])
                            tt(mg_st[:, s, :], ps_[:, 0:16], gateb[:, l, :], ALU.add, [pn_, 'gateb'], ['mg_st'])
                    for s in range(T // 128):
                        for kind in range(4):
                            G.append(lambda s=s, kind=kind: g_tok(s, kind))
                    return G

                def stores(si):
                    t0, T, j = slabs[si]
                    nsub = T // 128
                    tok = slice(t0, t0 + T)

                    def tokmaj(dst):
                        return dst[tok, :].rearrange("(s p) c -> p s c", p=128)
                    S.dma('pool', tokmaj(m_q), mqk_st[:, 0:nsub, 0:256], reads=['mqk_st'])
                    S.dma('pool', tokmaj(m_k), mqk_st[:, 0:nsub, 256:512], reads=['mqk_st'])
                    S.dma('pool', tokmaj(m_v1), mv_st[:, 0:nsub].rearrange("p s h e -> p s (h e)"), reads=['mv_st'])
                    S.dma('pool', tokmaj(m_o), mo_st[:, 0:nsub, :], reads=['mo_st'])
                    S.dma('pool', tokmaj(m_g), mg_st[:, 0:nsub, :], reads=['mg_st'])
                    S.dma('pool', tokmaj(n_v1), nv_st[:, 0:nsub].rearrange("p s h e -> p s (h e)"), reads=['nv_st'])
                    S.dma('pool', n_qT[:, tok].rearrange("(i p) t -> p i t", p=128), nq_st[:, :, 0:T], reads=['nq_st'])
                    S.dma('pool', n_kT[:, tok].rearrange("(i p) t -> p i t", p=128), nk_st[:, :, 0:T], reads=['nk_st'])
                    S.dma('pool', l_xT[:, tok].rearrange("(i p) t -> p i t", p=128), lx_st[:, :, 0:T], reads=['lx_st'])
                    S.dma('pool', l_gT[:, tok].rearrange("(i p) t -> p i t", p=128), lg_st[:, :, 0:T], reads=['lg_st'])

                norm1(0)
                for k in range(8):
                    S.dma('sp', W[:, k, :], wb_in[l, k * 128:(k + 1) * 128, :], writes=['wA'])
                S.op('dve', lambda e: e.memset(mv_st[:, :, :, 64:65], 1.0), writes=['mv_st'])
                S.op('dve', lambda e: e.memset(nv_st[:, :, :, 64:65], 1.0), writes=['nv_st'])
                norm2(0)
                norm3(0)
                for si in range(len(slabs)):
                    G = groups(si)
                    n = len(G)
                    nxt = si + 1 if si + 1 < len(slabs) else None
                    a, b = n // 4, n // 2
                    if nxt is not None:
                        norm1(nxt)
                    for g_ in G[:a]:
                        g_()
                    if nxt is not None:
                        norm2(nxt)
                    for g_ in G[a:b]:
                        g_()
                    if nxt is not None:
                        norm3(nxt)
                    for g_ in G[b:]:
                        g_()
                    stores(si)
                S.flush()

        SLABS9 = [(0, 256)] + [(256 + 512 * i, 512) for i in range(8)]

        def scan(out, d0, d1, init, reads, writes):
            return S.op('dve', lambda e: e.tensor_tensor_scan(out=out, data0=d0, data1=d1, initial=init, op0=ALU.mult, op1=ALU.add),
                        reads=reads, writes=writes)

        def phase_lru(l):
            with contextlib.ExitStack() as ph:
                X = sb(ph, 'lX', [128, 2, NTOK], BF16)
                G = sb(ph, 'lG', [128, 2, NTOK], BF16)
                XL = sb(ph, 'lXL', [128, NTOK], F32)
                XLb = sb(ph, 'lXLb', [128, NTOK], BF16)
                R = sb(ph, 'lR', [128, NTOK], F32)
                I_ = sb(ph, 'lI', [128, NTOK], F32)
                Bt = sb(ph, 'lBt', [128, NTOK], F32)
                HF = sb(ph, 'lHF', [128, NTOK], F32)
                HB = sb(ph, 'lHB', [128, NTOK], F32)
                Y = sb(ph, 'lY', [128, NTOK], BF16)
                WBf = sb(ph, 'lWBf', [128, 2, 2, 2, 128], F32)
                WB = sb(ph, 'lWB', [128, 2, 2, 2, 128], BF16)
                do_mod = (l + 1 < L)
                if do_mod:
                    wm = [sb(ph, 'lwm%d' % i, [128, 8, 512], F32) for i in range(2)]
                mod_i = [0]

                def mod_step():
                    if do_mod and mod_i[0] < 12:
                        mod_slab(l + 1, mod_i[0], wm, mod_i[0])
                        mod_i[0] += 1
                for i in range(2):
                    S.dma('sp', X[:, i, :], l_xT[i * 128:(i + 1) * 128, :], writes=['lX'])
                S.op('dve', lambda e: e.memset(WBf[:], 0.0), writes=['lWBf'])
                for d in range(2):
                    for ct in range(2):
                        for b in range(2):
                            for ax, src in ((0, lwa), (1, lwx)):
                                S.dma('sp', WBf[b * 64:(b + 1) * 64, d, ct, ax, b * 64:(b + 1) * 64], src[l, d, 2 * ct + b],
                                      reads=['lWBf'], writes=['lWBf%d%d%d%d' % (d, ct, b, ax)])
                allw = ['lWBf'] + ['lWBf%d%d%d%d' % (d, ct, b, ax) for d in range(2) for ct in range(2) for b in range(2) for ax in range(2)]
                cp(WB[:], WBf[:], allw, ['lWB'])
                for i in range(2):
                    S.dma('sp', G[:, i, :], l_gT[i * 128:(i + 1) * 128, :], writes=['lG'])
                for ct in range(2):
                    act(XL[:], X[:, ct, :], AF.Identity, ['lX', 'convw', 'convb'], ['lXL'],
                        scale=convw[:, l, ct, 2:3], bias=convb[:, l, ct:ct + 1])
                    for (s0, s1) in ((0, NCTX), (NCTX, NTOK)):
                        stt(XL[:, s0 + 2:s1], X[:, ct, s0:s1 - 2], convw[:, l, ct, 0:1], XL[:, s0 + 2:s1], ALU.mult, ALU.add,
                            ['lX', 'lXL', 'convw'], ['lXL'])
                        stt(XL[:, s0 + 1:s1], X[:, ct, s0:s1 - 1], convw[:, l, ct, 1:2], XL[:, s0 + 1:s1], ALU.mult, ALU.add,
                            ['lX', 'lXL', 'convw'], ['lXL'])
                        stt(XL[:, s0:s1 - 1], X[:, ct, s0 + 1:s1], convw[:, l, ct, 3:4], XL[:, s0:s1 - 1], ALU.mult, ALU.add,
                            ['lX', 'lXL', 'convw'], ['lXL'])
                    act(XLb[:], XL[:], AF.Copy, ['lXL'], ['lXLb'])
                    for d in range(2):
                        for (t0, T) in SLABS9:
                            ps, pn = next_ps()
                            mm(ps[:, 0:T], [(WB[:, d, ct, 0, :], XLb[:, t0:t0 + T])], pn, ['lXLb', 'lWB'])
                            act(R[:, t0:t0 + T], ps[:, 0:T], AF.Sigmoid, [pn, 'lba'], ['lR'], bias=lba[:, l, d, ct:ct + 1])
                            ps, pn = next_ps()
                            mm(ps[:, 0:T], [(WB[:, d, ct, 1, :], XLb[:, t0:t0 + T])], pn, ['lXLb', 'lWB'])
                            act(I_[:, t0:t0 + T], ps[:, 0:T], AF.Sigmoid, [pn, 'lbx'], ['lI'], bias=lbx[:, l, d, ct:ct + 1])
                            if t0 in (256, 1792, 3328):
                                mod_step()
                        act(Bt[:], R[:], AF.Exp, ['lR', 'cvec2'], ['lBt'], scale=cvec2[:, l, d, ct:ct + 1])
                        act(R[:], R[:], AF.Exp, ['lR', 'cvec'], ['lR'], scale=cvec[:, l, d, ct:ct + 1])
                        ts(Bt[:], Bt[:], -1.0, ALU.mult, ['lBt'], ['lBt'], s2=1.0, op1=ALU.add)
                        act(Bt[:], Bt[:], AF.Sqrt, ['lBt'], ['lBt'])
                        tt(Bt[:], Bt[:], I_[:], ALU.mult, ['lBt', 'lI'], ['lBt'])
                        tt(Bt[:], Bt[:], XL[:], ALU.mult, ['lBt', 'lXL'], ['lBt'])
                        if d == 0:
                            scan(HF[:], R[:], Bt[:], 0.0, ['lR', 'lBt'], ['lHF'])
                        else:
                            scan(HB[:, 0:NCTX][:, ::-1], R[:, 0:NCTX][:, ::-1], Bt[:, 0:NCTX][:, ::-1], 0.0, ['lR', 'lBt'], ['lHB'])
                            scan(HB[:, NCTX:NTOK][:, ::-1], R[:, NCTX:NTOK][:, ::-1], Bt[:, NCTX:NTOK][:, ::-1], HB[:, 0:1],
                                 ['lR', 'lBt', 'lHB'], ['lHB'])
                    tt(HF[:], HF[:], HB[:], ALU.add, ['lHF', 'lHB'], ['lHF'])
                    tt(Y[:], HF[:], G[:, ct, :], ALU.mult, ['lHF', 'lG'], ['lY'])
                    S.dma('pool', mixT[768 + 128 * ct:768 + 128 * (ct + 1), :], Y[:], reads=['lY'])
                while do_mod and mod_i[0] < 12:
                    mod_step()
                if do_mod:
                    mod_finish(l + 1)
                S.flush()

        def phase_C(l):
            last = (l == L - 1)
            xsrc = xT0 if l == 0 else xT
            with contextlib.ExitStack() as ph:
                Wo = sb(ph, 'cWo', [128, 8, D], BF16)
                W1 = sb(ph, 'cW1', [128, 8, MLPH], BF16)
                W2 = sb(ph, 'cW2', [128, 32, D], BF16)
                xts = [sb(ph, 'cxt%d' % i, [128, 8, 256], F32) for i in range(2)]
                mx = sb(ph, 'cmx', [128, 8, 256], BF16)
                sq = sb(ph, 'csq', [128, 8, 256], BF16)
                h2 = sb(ph, 'ch2', [128, 8, 256], BF16)
                hid = sb(ph, 'chid', [128, 32, 256], BF16)
                lnv = sb(ph, 'clnv', [128, 256], F32)
                rstd = sb(ph, 'crstd', [128, 256], F32)
                t1 = [sb(ph, 'ct1%d' % i, [128, 256], F32) for i in range(2)]
                rl = [sb(ph, 'crl%d' % i, [128, 256], F32) for i in range(2)]
                for k in range(8):
                    S.dma('sp', Wo[:, k, :], wb_out[l, k * 128:(k + 1) * 128, :], writes=['cWo'])
                T = 256
                tiles = [ti for ti in range(NTOK // T) if not (last and ti == 0)]

                def X(ti):
                    return xts[ti % 2], 'cxt%d' % (ti % 2)

                def load(ti):
                    xt, xn = X(ti)
                    tok = slice(ti * T, (ti + 1) * T)
                    S.dma('sp', mx[:], mixT[:, tok].rearrange("(k p) t -> p k t", p=128), writes=['cmx'])
                    S.dma('sp', xt[:], xsrc[:, tok].rearrange("(k p) t -> p k t", p=128), writes=[xn])

                def outproj(ti):
                    xt, xn = X(ti)
                    j = 1 if ti == 0 else 0
                    for m in range(8):
                        ps, pn = next_ps()
                        mm(ps[:, 0:T], [(Wo[:, k, m * 128:(m + 1) * 128], mx[:, k, :]) for k in range(8)], pn, ['cmx', 'cWo'])
                        stt(xt[:, m, :], ps[:, 0:T], modT[:, l, 16 + m, j:j + 1], xt[:, m, :], ALU.mult, ALU.add,
                            [pn, xn, 'modT'], [xn])

                ones_ps = {}

                def square(ti):
                    xt, xn = X(ti)
                    act(sq[:], xt[:], AF.Square, [xn], ['csq'])

                def onesmm(ti):
                    ps, pn = next_ps()
                    mm(ps[:, 0:T], [(onesb[:, :], sq[:, k, :]) for k in range(8)], pn, ['csq', 'onesb'])
                    ones_ps[ti] = (ps, pn)

                def normrest(ti):
                    xt, xn = X(ti)
                    j = 1 if ti == 0 else 0
                    ps, pn = ones_ps.pop(ti)
                    act(lnv[:], ps[:, 0:T], AF.Ln, [pn, 'c_eps'], ['clnv'], scale=1.0 / D, bias=c_eps[:, 0:1])
                    act(rstd[:], lnv[:], AF.Exp, ['clnv'], ['crstd'], scale=-0.5)
                    for k in range(8):
                        tb_ = t1[k % 2]; tn = 'ct1%d' % (k % 2)
                        stt(tb_[:], xt[:, k, :], A2v[:, l, j, k:k + 1], rstd[:], ALU.mult, ALU.mult, [xn, 'crstd', 'A2v'], [tn])
                        act(h2[:, k, :], tb_[:], AF.Identity, [tn, 'modT'], ['ch2'], bias=modT[:, l, 24 + k, j:j + 1], scale=1.0)

                def up(ti):
                    for m in range(32):
                        ps, pn = next_ps()
                        mm(ps[:, 0:T], [(W1[:, k, m * 128:(m + 1) * 128], h2[:, k, :]) for k in range(8)], pn, ['ch2', 'cW1'])
                        rb = rl[m % 2]; rn = 'crl%d' % (m % 2)
                        act(rb[:], ps[:, 0:T], AF.Relu, [pn], [rn])
                        tt(hid[:, m, :], rb[:], rb[:], ALU.mult, [rn], ['chid'])

                def down(ti, half):
                    xt, xn = X(ti)
                    j = 1 if ti == 0 else 0
                    for m in range(4 * half, 4 * half + 4):
                        ps, pn = next_ps()
                        mm(ps[:, 0:T], [(W2[:, k, m * 128:(m + 1) * 128], hid[:, k, :]) for k in range(32)], pn, ['chid', 'cW2'])
                        stt(xt[:, m, :], ps[:, 0:T], modT[:, l, 40 + m, j:j + 1], xt[:, m, :], ALU.mult, ALU.add,
                            [pn, xn, 'modT'], [xn])

                def store(ti):
                    xt, xn = X(ti)
                    if last:
                        dst = outT[:, ti * T - NCTX:(ti + 1) * T - NCTX]
                    else:
                        dst = xT[:, ti * T:(ti + 1) * T]
                    S.dma('sp', dst.rearrange("(k p) t -> p k t", p=128), xt[:], reads=[xn])

                load(tiles[0])
                for k in range(8):
                    S.dma('sp', W1[:, k, :], wb_1[l, k * 128:(k + 1) * 128, :], writes=['cW1'])
                for g_ in range(4):
                    S.dma('sp', W2[:, 8 * g_:8 * g_ + 8, :], wb_2[l, 1024 * g_:1024 * (g_ + 1), :].rearrange("(k p) n -> p k n", p=128),
                          writes=['cW2'])
                if l + 1 < L:
                    cast_weights(l + 1)
                prev = None
                for ti in tiles:
                    if prev is not None:
                        load(ti)
                    outproj(ti)
                    square(ti)
                    if prev is not None:
                        down(prev, 0)
                    onesmm(ti)
                    normrest(ti)
                    if prev is not None:
                        down(prev, 1)
                        store(prev)
                    up(ti)
                    prev = ti
                down(prev, 0)
                down(prev, 1)
                store(prev)
                S.flush()

        def phase_na(l):
            want_ctx = l < L - 1
            with contextlib.ExitStack() as ph:
                Q = sb(ph, 'nQ', [128, 4, NTOK], BF16)
                Kt = sb(ph, 'nK', [128, 4, NTOK], BF16)
                V = sb(ph, 'nV', [128, NCH, 520], BF16)
                BT = sb(ph, 'nBT', [128, nslots, 8, 128], BF16)
                PT = [sb(ph, 'nPT%d' % i, [128, 7 * 128], BF16) for i in range(3)]
                Yst = [sb(ph, 'nY%d' % i, [128, 512], BF16) for i in range(2)]
                rr = [sb(ph, 'nrr%d' % i, [128, 4], F32) for i in range(2)]
                MIXs = [sb(ph, 'nMIX%d' % i, [128, 4, 512], BF16) for i in range(2)]
                Qz = [sb(ph, 'nQz%d' % i, [128, 8, 128], BF16) for i in range(2)]
                for i in range(2):
                    S.op('pool', lambda e, i=i: e.memset(Qz[i][:], 0.0), writes=['nQz%d' % i])
                for i in range(4):
                    S.dma('sp', Q[:, i, :], n_qT[i * 128:(i + 1) * 128, :], writes=['nQ%d' % i])
                S.dma('sp', Kt[:, 0, :], n_kT[0:128, :], writes=['nK0'])
                S.dma('sp', V[:, 0:17, :], n_v1[0:17 * 128, :].rearrange("(c p) e -> p c e", p=128), writes=['nV0'])
                for i in range(1, 4):
                    S.dma('sp', Kt[:, i, :], n_kT[i * 128:(i + 1) * 128, :], writes=['nK%d' % i])
                S.dma('sp', V[:, 17:34, :], n_v1[17 * 128:34 * 128, :].rearrange("(c p) e -> p c e", p=128), writes=['nV1'])
                BTs = [sb(ph, 'nBTs%d' % i, [128, 8, 128], F32) for i in range(2)]
                btw = []
                si = 0
                for sig, slot0 in cases.items():
                    for i, (dk, blocks) in enumerate(sig):
                        stg = BTs[si % 2]; sn = 'nBTs%d' % (si % 2); si += 1
                        S.op('dve', lambda e, stg=stg: e.memset(stg[:], NEG), writes=[sn])
                        names = []
                        for bi, dr in enumerate(blocks):
                            if dr is None:
                                continue
                            b, a = bi // 2, bi % 2
                            nm = sn + '_%d' % bi
                            S.dma('sp', stg[b * 64:(b + 1) * 64, :, a * 64:(a + 1) * 64], TB[l, :, dr, :, :], reads=[sn], writes=[nm])
                            names.append(nm)
                        nm2 = 'nBT_%d' % (slot0 + i)
                        if si % 2 == 0:
                            act(BT[:, slot0 + i], stg[:], AF.Copy, [sn] + names, [nm2])
                        else:
                            cp(BT[:, slot0 + i], stg[:], [sn] + names, [nm2])
                        S.res[sn][1].append(S.res[nm2][0])
                        btw.append(nm2)
                qtiles = [(2 + j, plans[j]) for j in range(32)]
                if want_ctx:
                    qtiles = [(0, None), (1, None)] + qtiles
                units = []
                for qi, (tq, plan) in enumerate(qtiles):
                    if plan is None:
                        tiles = [(0, None), (1, None)]
                    else:
                        kts, sig = plan
                        slot0 = cases[sig]
                        tiles = [(2 + kt, slot0 + i) for i, kt in enumerate(kts)] + [(0, None), (1, None)]
                    for h in range(8):
                        units.append((qi, tq, tiles, h))
                state = {}

                def emit_qk(u):
                    qi, tq, tiles, h = units[u]
                    qsl = slice(tq * 128, (tq + 1) * 128)
                    qz = Qz[qi % 2]; qzn = 'nQz%d' % (qi % 2)
                    if h == 0:
                        cp(qz[0:64, 0:8:2, :], Q[0:64, :, qsl], ['nQ0', 'nQ1', 'nQ2', 'nQ3'], [qzn], eng='pool')
                        cp(qz[64:128, 1:8:2, :], Q[64:128, :, qsl], ['nQ0', 'nQ1', 'nQ2', 'nQ3'], [qzn], eng='pool')
                    n = len(tiles)
                    hp = h // 2
                    psA, pnA = next_ps()
                    psB, pnB = (next_ps() if n > 4 else (None, None))
                    for i, (tk, slot) in enumerate(tiles):
                        reg = (psA if i < 4 else psB)[:, (i % 4) * 128:(i % 4 + 1) * 128]
                        pn = pnA if i < 4 else pnB
                        pairs = [(Kt[:, hp, tk * 128:(tk + 1) * 128], qz[:, h, :])]
                        if slot is not None:
                            pairs.append((identb, BT[:, slot, h, :]))
                        mm(reg, pairs, pn, ['nK%d' % hp, qzn, 'cb'] + (['nBT_%d' % slot] if slot is not None else []))
                    state[u] = (psA, pnA, psB, pnB)

                def emit_rest(u):
                    qi, tq, tiles, h = units[u]
                    psA, pnA, psB, pnB = state.pop(u)
                    n = len(tiles)
                    nA = min(n, 4)
                    yb = Yst[qi % 2]; yn = 'nY%d' % (qi % 2)
                    pb = PT[u % 3]; pnm = 'nPT%d' % (u % 3)
                    act(pb[:, 0:nA * 128], psA[:, 0:nA * 128], AF.Exp, [pnA], [pnm + 'a'])
                    if n > 4:
                        act(pb[:, 512:n * 128], psB[:, 0:(n - 4) * 128], AF.Exp, [pnB], [pnm + 'b'])
                    hq = h % 4
                    if hq == 0:
                        state['psO'] = next_ps()
                    psO, pnO = state['psO']
                    mm(psO[:, hq * 65:(hq + 1) * 65],
                       [(pb[:, i * 128:(i + 1) * 128], V[:, tk, h * 65:(h + 1) * 65]) for i, (tk, _) in enumerate(tiles)],
                       pnO, [pnm + 'a', pnm + 'b', 'nV0', 'nV1'])
                    if hq == 3:
                        half = h // 4
                        rb = rr[half]; rn = 'nrr%d' % half
                        o3 = psO[:, 0:260].rearrange("p (h e) -> p h e", e=65)
                        S.op('dve', lambda e, rb=rb, o3=o3: e.reciprocal(out=rb[:], in_=o3[:, :, 64]), reads=[pnO], writes=[rn])
                        tt(yb[:, half * 256:(half + 1) * 256].rearrange("p (h d) -> p h d", d=64), o3[:, :, 0:64],
                           bc(rb[:], 1, 64), ALU.mult, [pnO, rn], [yn])
                    if h == 7:
                        for i in range(4):
                            S.op('pe', lambda e, i=i, yb=yb: e.transpose(out=PST[:, i * 128:(i + 1) * 128], in_=yb[:, i * 128:(i + 1) * 128], identity=identb),
                                 reads=[yn, 'cb'], writes=['pst'])
                        mb = MIXs[(tq // 4) % 2]; mn = 'nMIX%d' % ((tq // 4) % 2)
                        cp(mb[:, :, (tq % 4) * 128:(tq % 4 + 1) * 128], PST[:, 0:512].rearrange("p (i t) -> p i t", t=128), ['pst'], [mn])
                        grp_last = (tq % 4 == 3) or (tq == 1) or (tq == NCH - 1)
                        if grp_last:
                            g0 = (tq // 4) * 4
                            g1 = tq + 1
                            if tq == 1:
                                S.dma('sp', mixT[256:768, 0:256].rearrange("(i p) t -> p i t", p=128), mb[:, :, 0:256], reads=[mn])
                            elif g0 == 0:
                                S.dma('sp', mixT[256:768, 256:512].rearrange("(i p) t -> p i t", p=128), mb[:, :, 256:512], reads=[mn])
                            else:
                                S.dma('sp', mixT[256:768, g0 * 128:g1 * 128].rearrange("(i p) t -> p i t", p=128),
                                      mb[:, :, 0:(g1 - g0) * 128], reads=[mn])

                emit_qk(0)
                emit_qk(1)
                for u in range(len(units)):
                    if u + 2 < len(units):
                        emit_qk(u + 2)
                    emit_rest(u)
                S.flush()

        def phase_mlstm(l):
            import math
            with contextlib.ExitStack() as ph:
                QK = sb(ph, 'mQK', [128, NCH, 512], BF16)
                V1 = sb(ph, 'mV1', [128, NCH, 260], BF16)
                G16 = sb(ph, 'mG', [128, NCH, 16], F32)
                RCs = [sb(ph, 'mRC%d' % i, [128, 2, 64], F32) for i in range(2)]
                RSs = [sb(ph, 'mRS%d' % i, [128, 2, 64], F32) for i in range(2)]
                SP = sb(ph, 'mSP', [128, NCH, 8], F32)
                EQ = sb(ph, 'mEQ', [128, 2, NCH, 4], F32)
                EK = sb(ph, 'mEK', [128, 2, NCH, 4], F32)
                TMPG = sb(ph, 'mTG', [128, NCH, 4], F32)
                Dd = sb(ph, 'mD', [128, 2, 2, NCH], F32)
                Dfull = sb(ph, 'mDf', [128, 65, NCH], F32)
                Cst = sb(ph, 'mC', [128, 2, 65, NCH], F32)
                CB = sb(ph, 'mCB', [128, 4, NCH, 65], BF16)
                QsT = sb(ph, 'mQsT', [128, 2, NTOK], BF16)
                KsT = sb(ph, 'mKsT', [128, 2, NTOK], BF16)
                Qs = sb(ph, 'mQs', [128, 4, 256], BF16)
                Ks = sb(ph, 'mKs', [128, 4, 256], BF16)
                t1 = sb(ph, 'mt1', [128, 2, 512], F32)
                t2 = sb(ph, 'mt2', [128, 512], F32)
                SpT = [sb(ph, 'mSpT%d' % i, [128, 4, 128], BF16) for i in range(2)]
                HFs = sb(ph, 'mHF', [128, NCH, 256], BF16)
                SOs = [sb(ph, 'mSO%d' % i, [128, 4, 256], BF16) for i in range(2)]
                def two(nm, shp, dt):
                    return [sb(ph, nm + str(i), shp, dt) for i in range(2)]
                dd2 = two('mdd', [128, 4], F32); rr2 = two('mrr', [128, 4], F32)
                hb2 = two('mhb', [128, 256], F32); hs2 = two('mhs', [128, 256], F32); sq22 = two('msq2', [128, 256], F32)
                ss2 = two('mss', [128, 4], F32); lnn2 = two('mlnn', [128, 4], F32); rsn2 = two('mrsn', [128, 4], F32)
                y12 = two('my1', [128, 256], F32); YM2 = two('mYM', [128, 256], BF16)
                YMB = two('mYMB', [128, 4, 256], BF16)
                ssB = two('mssB', [128, 16], F32); lnB = two('mlnB', [128, 16], F32); rsB = two('mrsB', [128, 16], F32)
                MIXs = [sb(ph, 'mMIX%d' % i, [128, 2, 512], BF16) for i in range(2)]
                c_ln8 = sb(ph, 'mln8', [128, 1], F32)
                mng = sb(ph, 'mng', [128, 256], F32)
                S.dma('sp', mng[:], mng_d[:, l, :], writes=['mng'])

                S.dma('sp', G16[:], m_g.rearrange("(c p) e -> p c e", p=128), writes=['mG'])
                S.dma('sp', QK[:, :, 0:256], m_q.rearrange("(c p) e -> p c e", p=128), writes=['mQKq'])
                S.dma('sp', QK[:, :, 256:512], m_k.rearrange("(c p) e -> p c e", p=128), writes=['mQKk'])
                S.dma('sp', V1[:], m_v1.rearrange("(c p) e -> p c e", p=128), writes=['mV1'])
                S.op('dve', lambda e: e.memset(c_ln8[:], math.log(0.125)), writes=['mln8'])
                S.op('dve', lambda e: e.memset(CB[:], 0.0), writes=['mCB'])
                act(SP[:], G16[:, :, 8:16], AF.Exp, ['mG'], ['mSP'], scale=-1.0)
                act(SP[:], SP[:], AF.Ln, ['mSP', 'c_one'], ['mSP'], bias=c_one[:, 0:1])
                sp2 = SP[:].rearrange("p c r -> p (c r)")
                psF, pnF = next_ps()
                mm(psF[:, 0:NCH * 8], [(constf[:, 1, :], sp2)], pnF, ['mSP', 'constf'])
                psB, pnB = next_ps()
                mm(psB[:, 0:NCH * 8], [(constf[:, 2, :], sp2)], pnB, ['mSP', 'constf'])
                psF3 = psF[:, 0:NCH * 8].rearrange("p (c r) -> p c r", r=8)
                psB3 = psB[:, 0:NCH * 8].rearrange("p (c r) -> p c r", r=8)
                act(EQ[:, 0], psF3[:, :, 0:4], AF.Exp, [pnF], ['mEQ'], scale=-1.0)
                act(EQ[:, 1], psB3[:, :, 4:8], AF.Exp, [pnB], ['mEQ'], scale=-1.0)
                tt(TMPG[:], G16[:, :, 0:4], psF3[:, :, 0:4], ALU.add, ['mG', pnF], ['mTG'])
                act(EK[:, 0], TMPG[:], AF.Exp, ['mTG', 'mln8'], ['mEK'], bias=c_ln8[:, 0:1])
                tt(TMPG[:], G16[:, :, 4:8], psB3[:, :, 4:8], ALU.add, ['mG', pnB], ['mTG'])
                act(EK[:, 1], TMPG[:], AF.Exp, ['mTG', 'mln8'], ['mEK'], bias=c_ln8[:, 0:1])
                psD0, pnD0 = next_ps()
                mm(psD0[:, 0:NCH * 4], [(constf[:, 3, :], EQ[:, 0].rearrange("p c h -> p (c h)"))], pnD0, ['mEQ', 'constf'])
                psD1, pnD1 = next_ps()
                mm(psD1[:, 0:NCH * 4], [(constf[:, 4, :], EQ[:, 1].rearrange("p c h -> p (c h)"))], pnD1, ['mEQ', 'constf'])
                d03 = psD0[:, 0:NCH * 4].rearrange("p (c h) -> p c h", h=4)
                d13 = psD1[:, 0:NCH * 4].rearrange("p (c h) -> p c h", h=4)
                for hp in range(2):
                    for half in range(2):
                        psl = slice(half * 64, (half + 1) * 64)
                        hd = 2 * hp + half
                        cp(Dd[psl, 0, hp, :], d03[psl, :, hd], [pnD0], ['mD'])
                        cp(Dd[psl, 1, hp, 0:2], d13[psl, 0:2, hd][:, ::-1], [pnD1], ['mD'])
                        cp(Dd[psl, 1, hp, 2:NCH], d13[psl, 2:NCH, hd][:, ::-1], [pnD1], ['mD'])
                import os
                MST = int(os.environ.get('M_STOP', '9'))
                for s2 in range(0):
                    c0 = 2 + 2 * s2
                    rc = RCs[s2 % 2]; rs = RSs[s2 % 2]
                    S.dma('sp', rc[:], ropec_d[:, 2 * s2:2 * s2 + 2, :], writes=['mRC%d' % (s2 % 2)])
                    S.dma('sp', rs[:], ropes_d[:, 2 * s2:2 * s2 + 2, :], writes=['mRS%d' % (s2 % 2)])
                    qk4 = QK[:, c0:c0 + 2, :].rearrange("p c (h d) -> p c h d", d=64)
                    tt(t1[:].rearrange("p c (h d) -> p c h d", d=64), qk4, bc(rc[:], 1, 8), ALU.mult,
                       ['mQKq', 'mQKk', 'mRC%d' % (s2 % 2)], ['mt1'])
                    for cc in range(2):
                        qv = QK[:, c0 + cc, :].rearrange("p (h f a e) -> p h f a e", h=8, f=2, a=2, e=16)
                        tv = t2[:].rearrange("p (h f a e) -> p h f a e", h=8, f=2, a=2, e=16)
                        rv = rs[:, cc, :].rearrange("p (f a e) -> p f a e", f=2, a=2, e=16)
                        for a in range(2):
                            tt(tv[:, :, :, a, :], qv[:, :, :, 1 - a, :], bc(rv[:, :, a, :], 0, 8), ALU.mult,
                               ['mQKq', 'mQKk', 'mRS%d' % (s2 % 2)], ['mt2'])
                        tt(QK[:, c0 + cc, :], t1[:, cc, :], t2[:], ALU.add, ['mt1', 'mt2'], ['mQKq', 'mQKk'])

                def pos_of(d, c):
                    if d == 0:
                        return c
                    return 1 - c if c < 2 else NCH + 1 - c

                store_i = [0]
                for d in range(2 if MST >= 6 else (1 if MST >= 3 else 0)):
                    for s4 in range(9):
                        c0 = 4 * s4
                        n4 = min(4, NCH - c0)
                        tt(Qs[:, 0:n4, :].rearrange("p c (h e) -> p c h e", e=64),
                           QK[:, c0:c0 + n4, 0:256].rearrange("p c (h e) -> p c h e", e=64),
                           bc(EQ[:, d, c0:c0 + n4, :], 2, 64), ALU.mult, ['mQKq', 'mEQ'], ['mQs'])
                        tt(Ks[:, 0:n4, :].rearrange("p c (h e) -> p c h e", e=64),
                           QK[:, c0:c0 + n4, 256:512].rearrange("p c (h e) -> p c h e", e=64),
                           bc(EK[:, d, c0:c0 + n4, :], 2, 64), ALU.mult, ['mQKk', 'mEK'], ['mKs'])
                        for cp0 in range(0, n4, 2):
                            npair = min(2, n4 - cp0)
                            for ccl in range(npair):
                                cc = cp0 + ccl
                                for i, (src, nm) in enumerate(((Qs, 'mQs'), (Qs, 'mQs'), (Ks, 'mKs'), (Ks, 'mKs'))):
                                    hp = i % 2
                                    S.op('pe', lambda e, i=i, src=src, cc=cc, hp=hp, ccl=ccl: e.transpose(
                                        out=PST[:, (ccl * 4 + i) * 128:(ccl * 4 + i + 1) * 128], in_=src[:, cc, hp * 128:(hp + 1) * 128], identity=identb),
                                        reads=[nm, 'cb'], writes=['pst'])
                            c_a = c0 + cp0
                            csl2 = slice(c_a * 128, (c_a + npair) * 128)
                            pv = PST[:, 0:npair * 512].rearrange("p (c j t) -> p c j t", j=4, t=128)
                            act(QsT[:, :, csl2].rearrange("p i (c t) -> p c i t", t=128), pv[:, :, 0:2, :], AF.Copy, ['pst'], ['mQsT'])
                            act(KsT[:, :, csl2].rearrange("p i (c t) -> p c i t", t=128), pv[:, :, 2:4, :], AF.Copy, ['pst'], ['mKsT'])
                            for ccl in range(npair):
                                cc = cp0 + ccl
                                c = c0 + cc
                                psP, pnP = next_ps()
                                for h in range(4):
                                    pr, hp = (h % 2) * 64, h // 2
                                    mm(psP[pr:pr + 64, hp * 65:(hp + 1) * 65],
                                       [(Ks[:, cc, h * 64:(h + 1) * 64], V1[:, c, h * 65:(h + 1) * 65])], pnP, ['mKs', 'mV1'])
                                pos = pos_of(d, c)
                                for hp in range(2):
                                    ts(Cst[:, hp, :, pos], psP[:, hp * 65:(hp + 1) * 65], Dd[:, d, hp, pos:pos + 1], ALU.mult,
                                       [pnP, 'mD'], ['mC'])
                    S.op('dve', lambda e, d=d: e.memset(Dd[:, d, :, 0:1], 0.0), reads=['mC'], writes=['mD'])
                    for hp in range(2):
                        cp(Dfull[:], bc(Dd[:, d, hp, :], 0, 65), ['mD'], ['mDf'])
                        cf = Cst[:, hp].rearrange("p e c -> p (e c)")
                        scan(cf, Dfull[:].rearrange("p e c -> p (e c)"), cf, 0.0, ['mDf', 'mC'], ['mC'])
                        for half in range(2):
                            psl = slice(half * 64, (half + 1) * 64)
                            cp(CB[psl, 2 * hp + half], Cst[psl, hp].rearrange("p e c -> p c e"), ['mC'], ['mCB'])
                    if MST < 5:
                        continue
                    if MST < 5:
                        continue
                    st3 = {}

                    def emit_S(c, d=d):
                        csl = slice(c * 128, (c + 1) * 128)
                        psS0, pnS0 = next_ps()
                        psS1, pnS1 = next_ps()
                        for h in range(4):
                            pr, hp = (h % 2) * 64, h // 2
                            bank, pnS = (psS0, pnS0) if h % 2 == 0 else (psS1, pnS1)
                            mm(bank[:, hp * 128:(hp + 1) * 128], [(KsT[pr:pr + 64, hp, csl], QsT[pr:pr + 64, hp, csl])], pnS, ['mKsT', 'mQsT'])
                        spb = SpT[c % 2]; spn = 'mSpT%d' % (c % 2)
                        tt(spb[:, 0:4:2, :], psS0[:, 0:256].rearrange("p (h t) -> p h t", t=128), bc(cb[:, 1 + d, :], 0, 2), ALU.mult,
                           [pnS0, 'cb'], [spn])
                        tt(spb[:, 1:4:2, :], psS1[:, 0:256].rearrange("p (h t) -> p h t", t=128), bc(cb[:, 1 + d, :], 0, 2), ALU.mult,
                           [pnS1, 'cb'], [spn])

                    def emit_H(c, d=d):
                        csl = slice(c * 128, (c + 1) * 128)
                        pos = pos_of(d, c)
                        b2 = c % 2
                        sfx = str(b2)
                        spb = SpT[b2]; spn = 'mSpT%d' % b2
                        dd, rr, hb, hs, sq2, ss, lnn, rsn, y1, YM = (dd2[b2], rr2[b2], hb2[b2], hs2[b2], sq22[b2], ss2[b2], lnn2[b2],
                                                                    rsn2[b2], y12[b2], YM2[b2])
                        psH, pnH = next_ps()
                        for h in range(4):
                            hp = h // 2
                            pairs = [(spb[:, h, :], V1[:, c, h * 65:(h + 1) * 65])]
                            if pos > 0:
                                pairs.append((QsT[:, hp, csl], CB[:, h, pos - 1, :]))
                            mm(psH[:, h * 65:(h + 1) * 65], pairs, pnH, [spn, 'mV1', 'mQsT', 'mCB'])
                        H3 = psH[:, 0:260].rearrange("p (h e) -> p h e", e=65)
                        act(dd[:], H3[:, :, 64], AF.Abs, [pnH], ['mdd' + sfx])
                        ts(dd[:], dd[:], 1.0, ALU.max, ['mdd' + sfx], ['mdd' + sfx])
                        S.op('dve', lambda e: e.reciprocal(out=rr[:], in_=dd[:]), reads=['mdd' + sfx], writes=['mrr' + sfx])
                        if d == 0:
                            tt(HFs[:, c, :].rearrange("p (h e) -> p h e", e=64), H3[:, :, 0:64], bc(rr[:], 1, 64), ALU.mult,
                               [pnH, 'mrr' + sfx], ['mHF'])
                            return
                        tt(hb[:].rearrange("p (h e) -> p h e", e=64), H3[:, :, 0:64], bc(rr[:], 1, 64), ALU.mult, [pnH, 'mrr' + sfx], ['mhb' + sfx])
                        tt(HFs[:, c, :], hb[:], HFs[:, c, :], ALU.add, ['mhb' + sfx, 'mHF'], ['mHF'], eng='pool')

                    def emit_B(g, d=d):
                        c0 = 4 * g
                        n4 = min(4, NCH - c0)
                        b2 = g % 2
                        sfx = 'B%d' % b2
                        sob = SOs[b2]; son = 'mSO%d' % b2
                        S.dma('sp', sob[:, 0:n4, :], m_o[c0 * 128:(c0 + n4) * 128, :].rearrange("(c p) e -> p c e", p=128), writes=[son])
                        sqb = (Qs, Ks)[b2]; sqn = ('mQs', 'mKs')[b2]
                        y1b = Dfull[:].rearrange("p e c -> p (e c)")[:, b2 * 1024:(b2 + 1) * 1024]
                        y1n = 'mDf%d' % b2
                        ymb = YMB[b2]; ymn = 'mYMB%d' % b2
                        ssb, lnb, rsb = ssB[b2], lnB[b2], rsB[b2]
                        HS = HFs[:, c0:c0 + n4, :]
                        tt(sqb[:, 0:n4, :], HS, HS, ALU.mult, ['mHF'], [sqn], eng='pool')
                        S.op('dve', lambda e: e.tensor_reduce(out=ssb[:, 0:n4 * 4], in_=sqb[:, 0:n4, :].rearrange("p c (h e) -> p (c h) e", e=64),
                                                              axis=AX.X, op=ALU.add), reads=[sqn], writes=['mss' + sfx])
                        act(lnb[:, 0:n4 * 4], ssb[:, 0:n4 * 4], AF.Ln, ['mss' + sfx, 'c_eps'], ['mln' + sfx], scale=1.0 / HD, bias=c_eps[:, 0:1])
                        act(rsb[:, 0:n4 * 4], lnb[:, 0:n4 * 4], AF.Exp, ['mln' + sfx], ['mrs' + sfx], scale=-0.5)
                        y3 = y1b[:, 0:n4 * 256].rearrange("p (g e) -> p g e", e=64)
                        tt(y3, HS.rearrange("p c (h e) -> p (c h) e", e=64), bc(rsb[:, 0:n4 * 4], 1, 64), ALU.mult, ['mHF', 'mrs' + sfx], [y1n])
                        y4 = y1b[:, 0:n4 * 256].rearrange("p (c f) -> p c f", f=256)
                        tt(y4, y4, bc(mng[:], 0, n4), ALU.mult, [y1n, 'mng'], [y1n])
                        tt(ymb[:, 0:n4, :], y4, sob[:, 0:n4, :], ALU.mult, [y1n, son], [ymn], eng='pool')

                    def emit_B2(g, d=d):
                        c0 = 4 * g
                        n4 = min(4, NCH - c0)
                        b2 = g % 2
                        ymb = YMB[b2]; ymn = 'mYMB%d' % b2
                        for cc in range(n4):
                            for hp in range(2):
                                S.op('pe', lambda e, cc=cc, hp=hp, ymb=ymb: e.transpose(
                                    out=PST[:, (cc * 2 + hp) * 128:(cc * 2 + hp + 1) * 128], in_=ymb[:, cc, hp * 128:(hp + 1) * 128], identity=identb),
                                    reads=[ymn, 'cb'], writes=['pst'])
                        mb = MIXs[b2]; mn = 'mMIX%d' % b2
                        act(mb[:, :, 0:n4 * 128].rearrange("p i (c t) -> p c i t", t=128),
                            PST[:, 0:n4 * 256].rearrange("p (c i t) -> p c i t", i=2, t=128), AF.Copy, ['pst'], [mn])
                        S.dma('pool', mixT[0:256, c0 * 128:(c0 + n4) * 128].rearrange("(i p) t -> p i t", p=128),
                              mb[:, :, 0:n4 * 128], reads=[mn])

                    emit_S(0)
                    for c in range(NCH):
                        if c + 1 < NCH:
                            emit_S(c + 1)
                        emit_H(c)
                        if d == 1 and (c % 4 == 3 or c == NCH - 1):
                            g_ = c // 4
                            emit_B(g_)
                            if g_ >= 1:
                                emit_B2(g_ - 1)
                            if c == NCH - 1:
                                emit_B2(g_)
                S.flush()

        def zero_rows(r0, r1):
            with contextlib.ExitStack() as ph:
                z = sb(ph, 'zz', [128, NTOK], BF16)
                S.op('dve', lambda e: e.memset(z[:], 0.0), writes=['zz'])
                for i in range(r0, r1):
                    S.dma('sp', mixT[i * 128:(i + 1) * 128, :], z[:], reads=['zz'])
                S.flush()

        import os
        PH = os.environ.get('K_PHASES', 'ALMNC')
        for l in range(L):
            phase_A(l)
            if stop_after == ('A', l):
                break
            if 'L' in PH:
                phase_lru(l)
            if 'M' in PH:
                phase_mlstm(l)
            else:
                zero_rows(0, 2)
            if 'N' in PH:
                phase_na(l)
            else:
                zero_rows(2, 6)
            if 'C' in PH:
                phase_C(l)
        S.flush()
    return nc


_NC_CACHE = {}


def kernel(**inputs):
    depth = inputs['w_in'].shape[0]
    maps = prep_inputs(inputs, depth)
    if depth not in _NC_CACHE:
        _NC_CACHE[depth] = build(depth)
    nc = _NC_CACHE[depth]
    res = run_bass_kernel_spmd(nc, maps, core_ids=list(range(len(maps))))
    out = np.stack([np.ascontiguousarray(r['outT'].T) for r in res.results], axis=0)
    return out.astype(np.float32)
```

```python
import contextlib
import numpy as np
import concourse.bass as bass
import concourse.mybir as mybir
from concourse.bass_utils import run_bass_kernel_spmd

F32 = mybir.dt.float32
BF16 = mybir.dt.bfloat16
AF = mybir.ActivationFunctionType
ALU = mybir.AluOpType
AX = mybir.AxisListType

D = 1024
NCTX = 256
NLAT = 4096
NTOK = NCTX + NLAT
NCH = NTOK // 128
GRID_W = 64
HD = 64
EPS = 1e-6
NEG = -30000.0
MLPH = 4096
C_NQ, C_NK, C_LX, C_LG, C_MQ, C_MK, C_MV, C_MO, C_NV, C_GI, C_GF = 0, 512, 1024, 1280, 1536, 1792, 2048, 2304, 2560, 3072, 3080
NCOL = 3088


class Sched:
    ENG = ('pe', 'act', 'dve', 'pool', 'sp')

    def __init__(self, nc, stack, n_dma_sems=48):
        self.nc = nc
        self.q = {e: [] for e in self.ENG}
        self.cnt = {e: 0 for e in self.ENG}
        self.sem = {e: stack.enter_context(nc.semaphore("s_" + e)) for e in ('pe', 'act', 'dve', 'pool')}
        self.dsem = [[stack.enter_context(nc.semaphore("d%d" % i)), 0] for i in range(n_dma_sems)]
        self.dpool = {'sp': list(range(0, n_dma_sems // 2)), 'pool': list(range(n_dma_sems // 2, n_dma_sems)),
                      'act': list(range(0, n_dma_sems // 2))}
        self.drr = {'sp': 0, 'pool': 0, 'act': 0}
        self.seen = {}
        self.res = {}
        self.ninstr = 0

    def _semh(self, key):
        return self.sem[key] if isinstance(key, str) else self.dsem[key[1]][0]

    def _wait(self, eng, tok):
        key, val = tok
        if self.seen.get((eng, key), 0) >= val:
            return
        self.seen[(eng, key)] = val
        h = self._semh(key)
        self.q[eng].append(lambda e, h=h, val=val: e.wait_ge(h, val))

    def _deps(self, eng, reads, writes):
        toks = []
        for r in reads:
            st = self.res.get(r)
            if st and st[0]:
                toks.append((st[0], 'raw'))
            if st and isinstance(r, str) and r.startswith('ps'):
                for t in st[1]:
                    if t[0] != eng:
                        toks.append((t, 'rar'))
        for w in writes:
            st = self.res.get(w)
            if st:
                if st[0]:
                    toks.append((st[0], 'waw'))
                for t in st[1]:
                    toks.append((t, 'war'))
        for tok, kind in toks:
            key, val = tok
            if key == eng:
                if eng == 'pe':
                    continue
                if eng != 'pool' and self.cnt[eng] - val >= 8:
                    continue
            self._wait(eng, tok)

    def _commit(self, tok, reads, writes):
        for r in reads:
            st = self.res.setdefault(r, [None, []])
            st[1].append(tok)
            if len(st[1]) > 48:
                best = {}
                for k, v in st[1]:
                    if best.get(k, 0) < v:
                        best[k] = v
                st[1] = list(best.items())
        for w in writes:
            self.res[w] = [tok, []]

    def op(self, eng, fn, reads=(), writes=()):
        self._deps(eng, reads, writes)
        self.cnt[eng] += 1
        h = self.sem[eng]
        self.q[eng].append(lambda e, fn=fn, h=h: fn(e).then_inc(h, 1))
        tok = (eng, self.cnt[eng])
        self._commit(tok, reads, writes)
        self.ninstr += 1
        return tok

    def op_nosig(self, eng, fn, reads=(), writes=()):
        self._deps(eng, reads, writes)
        self.q[eng].append(lambda e, fn=fn: fn(e))
        self.ninstr += 1

    def dma(self, eng, out, in_, reads=(), writes=(), **kw):
        self._deps(eng, reads, writes)
        pool_ = self.dpool[eng]
        idx = pool_[self.drr[eng] % len(pool_)]
        self.drr[eng] += 1
        ent = self.dsem[idx]
        key = ('dma', idx)
        if ent[1] > 0:
            self._wait(eng, (key, ent[1]))
        ent[1] += 16
        h = ent[0]
        self.q[eng].append(lambda e, h=h, out=out, in_=in_, kw=kw: e.dma_start(out=out, in_=in_, **kw).then_inc(h, 16))
        tok = (key, ent[1])
        self._commit(tok, reads, writes)
        self.ninstr += 1
        return tok

    def barrier(self):
        for e in self.ENG:
            for i, ent in enumerate(self.dsem):
                if ent[1] > 0:
                    self._wait(e, (('dma', i), ent[1]))
            for o in ('pe', 'act', 'dve', 'pool'):
                if o != e and self.cnt[o] > 0:
                    self._wait(e, (o, self.cnt[o]))
        self.res = {}

    def flush(self):
        self.barrier()
        nc = self.nc
        q = self.q
        with nc.Block() as block:
            @block.tensor
            def _(e):
                for f in q['pe']:
                    f(e)

            @block.scalar
            def _(e):
                for f in q['act']:
                    f(e)

            @block.vector
            def _(e):
                for f in q['dve']:
                    f(e)

            @block.gpsimd
            def _(e):
                for f in q['pool']:
                    f(e)

            @block.sync
            def _(e):
                for f in q['sp']:
                    f(e)
        self.q = {e: [] for e in self.ENG}


def na_plan():
    rows, wr = 64, 8

    def sr(r):
        return min(max(r - 4, 0), rows - wr)
    plans = []
    for j in range(32):
        kts = sorted(set((sr(2 * j + a) + i) // 2 for a in (0, 1) for i in range(8)))
        entry = []
        for kt in kts:
            blocks = []
            for b in (0, 1):
                for a in (0, 1):
                    r, rp = 2 * j + a, 2 * kt + b
                    s0 = sr(r)
                    blocks.append(rp - r + 7 if s0 <= rp < s0 + 8 else None)
            entry.append((kt - j, tuple(blocks)))
        plans.append((kts, tuple(entry)))
    cases = {}
    off = 0
    for kts, sig in plans:
        if sig not in cases:
            cases[sig] = off
            off += len(sig)
    return plans, cases, off


def _bf_layout_small(a, L):
    return np.ascontiguousarray(a)


def prep_inputs(inp, depth):
    L = depth
    f32 = np.float32
    g = {k: np.asarray(v, dtype=f32) for k, v in inp.items()}
    B = g['x'].shape[0]
    m0, na0, l0 = 0, 1040, 2576
    gi = [1024 + i for i in (0, 1, 2, 3, 8, 9, 10, 11)]
    gf = [1024 + i for i in (4, 5, 6, 7, 12, 13, 14, 15)]
    cols = (list(range(na0, na0 + 512)) + list(range(na0 + 512, na0 + 1024)) +
            list(range(l0, l0 + 256)) + list(range(l0 + 256, l0 + 512)) +
            list(range(0, 256)) + list(range(256, 512)) + list(range(512, 768)) + list(range(768, 1024)) +
            list(range(na0 + 1024, na0 + 1536)) + gi + gf)
    assert len(cols) == NCOL
    shared = {}
    shared['w_in'] = np.ascontiguousarray(g['w_in'][:L][:, :, cols])
    shared['w_mod'] = np.ascontiguousarray(g['w_mod'][:L])
    shared['w_out'] = np.ascontiguousarray(g['w_out'][:L])
    shared['w1'] = np.ascontiguousarray(g['w_mlp1'][:L])
    shared['w2'] = np.ascontiguousarray(g['w_mlp2'][:L])
    shared['bmodT'] = np.ascontiguousarray(g['b_mod'][:L].reshape(L, 48, 128).transpose(2, 0, 1))
    shared['n1T'] = np.ascontiguousarray(g['norm1_g'][:L].reshape(L, 8, 128).transpose(2, 0, 1))
    shared['n2T'] = np.ascontiguousarray(g['norm2_g'][:L].reshape(L, 8, 128).transpose(2, 0, 1))
    gb = g['mlstm_gate_b'][:L]
    gtok = np.concatenate([gb[:, 0], gb[:, 2], gb[:, 1], gb[:, 3]], axis=1)
    shared['gateb'] = np.ascontiguousarray(np.broadcast_to(gtok[None], (128, L, 16)))
    shared['mng'] = np.ascontiguousarray(np.broadcast_to(g['mlstm_norm_g'][:L][None], (128, L, 256)))
    shared['nqg'] = np.ascontiguousarray(np.tile(g['na_q_norm_g'][:L], (1, 2)).T)
    shared['nkg'] = np.ascontiguousarray(np.tile(g['na_k_norm_g'][:L], (1, 2)).T)
    cc = np.arange(64)
    col0 = np.clip(cc - 8, 0, 48)
    cp = np.arange(64)
    valid = (cp[:, None] >= col0[None, :]) & (cp[:, None] < col0[None, :] + 16)
    didx = np.clip(cp[:, None] - cc[None, :] + 15, 0, 30)
    rpb = g['na_rpb'][:L]
    tb = rpb[:, :, :, didx]
    tb = np.where(valid[None, None, None], tb, f32(NEG))
    shared['TB'] = np.ascontiguousarray(tb.transpose(0, 3, 2, 1, 4)).astype(f32)
    shared['convw'] = np.ascontiguousarray(g['lru_conv_w'][:L].reshape(L, 4, 2, 128).transpose(3, 0, 2, 1))
    shared['convb'] = np.ascontiguousarray(g['lru_conv_b'][:L].reshape(L, 2, 128).transpose(2, 0, 1))
    for nm, key in (('lba', 'lru_b_a'), ('lbx', 'lru_b_x'), ('llam', 'lru_lambda')):
        shared[nm] = np.ascontiguousarray(g[key][:L].reshape(L, 2, 2, 128).transpose(3, 0, 1, 2))
    shared['lwa'] = np.ascontiguousarray(g['lru_w_a'][:L])
    shared['lwx'] = np.ascontiguousarray(g['lru_w_x'][:L])
    t = np.arange(NLAT)
    row = (t // GRID_W).astype(f32)
    col = (t % GRID_W).astype(f32)
    inv = (f32(10000.0) ** (-np.arange(16, dtype=f32) / f32(16))).astype(f32)
    ar = row[:, None] * inv
    ac = col[:, None] * inv
    ang = np.concatenate([ar, ar, ac, ac], axis=-1)
    sgn = np.concatenate([-np.ones(16), np.ones(16), -np.ones(16), np.ones(16)]).astype(f32)
    cos = np.cos(ang).astype(f32)
    sin = (np.sin(ang) * sgn).astype(f32)
    shared['ropec'] = np.ascontiguousarray(cos.reshape(32, 128, 64).transpose(1, 0, 2))
    shared['ropes'] = np.ascontiguousarray(sin.reshape(32, 128, 64).transpose(1, 0, 2))
    i128 = np.arange(128)
    consts = np.zeros((128, 6, 128), f32)
    consts[:, 0, :] = np.eye(128)
    consts[:, 1, :] = (i128[:, None] <= i128[None, :])
    consts[:, 2, :] = (i128[:, None] >= i128[None, :])
    consts[127, 3, :] = 1.0
    consts[0, 4, :] = 1.0
    consts[:, 5, :] = (i128[:, None] // 64 == i128[None, :] // 64)
    shared['consts'] = consts
    maps = []
    for b in range(B):
        m = dict(shared)
        m['xT0'] = np.ascontiguousarray(np.concatenate([g['ctx'][b].T, g['x'][b].T], axis=1))
        cT = np.stack([g['c'][b].reshape(8, 128).T, g['c_ctx'].reshape(8, 128).T], axis=-1)
        m['cT'] = np.ascontiguousarray(cT)
        maps.append(m)
    return maps


def bc(ap, pos, n):
    dims = [list(d) for d in ap.ap]
    dims.insert(1 + pos, [0, n])
    return bass.AP(tensor=ap.tensor, offset=ap.offset, ap=dims)


class Ctx:
    pass


def build(depth, dbg=(), stop_after=None):
    nc = bass.Bass("TRN2", target_bir_lowering=False)
    L = depth
    plans, cases, nslots = na_plan()

    def din(name, shape):
        return nc.dram_tensor(name, list(shape), F32, kind="ExternalInput").ap()

    def dscr(name, shape, dt):
        return nc.dram_tensor(name, list(shape), dt, kind=("ExternalOutput" if name in dbg else "Internal")).ap()

    xT0 = din('xT0', [D, NTOK]); cT = din('cT', [128, 8, 2])
    w_in = din('w_in', [L, D, NCOL]); w_mod = din('w_mod', [L, D, 6 * D]); w_out = din('w_out', [L, D, D])
    w1 = din('w1', [L, D, MLPH]); w2 = din('w2', [L, MLPH, D])
    bmodT_d = din('bmodT', [128, L, 48]); n1T_d = din('n1T', [128, L, 8]); n2T_d = din('n2T', [128, L, 8])
    gateb_d = din('gateb', [128, L, 16]); mng_d = din('mng', [128, L, 256])
    nqg_d = din('nqg', [128, L]); nkg_d = din('nkg', [128, L])
    TB = din('TB', [L, 64, 15, 8, 64])
    convw_d = din('convw', [128, L, 2, 4]); convb_d = din('convb', [128, L, 2])
    lba_d = din('lba', [128, L, 2, 2]); lbx_d = din('lbx', [128, L, 2, 2]); llam_d = din('llam', [128, L, 2, 2])
    lwa = din('lwa', [L, 2, 4, 64, 64]); lwx = din('lwx', [L, 2, 4, 64, 64])
    ropec_d = din('ropec', [128, 32, 64]); ropes_d = din('ropes', [128, 32, 64])
    consts_d = din('consts', [128, 6, 128])
    outT = nc.dram_tensor('outT', [D, NLAT], F32, kind="ExternalOutput").ap()

    xT = dscr('xT', [D, NTOK], F32)
    wb_in = dscr('wb_in', [L, D, NCOL], BF16); wb_out = dscr('wb_out', [L, D, D], BF16)
    wb_1 = dscr('wb_1', [L, D, MLPH], BF16); wb_2 = dscr('wb_2', [L, MLPH, D], BF16)
    m_q = dscr('m_q', [NTOK, 256], BF16); m_k = dscr('m_k', [NTOK, 256], BF16)
    m_v1 = dscr('m_v1', [NTOK, 260], BF16); m_o = dscr('m_o', [NTOK, 256], BF16); m_g = dscr('m_g', [NTOK, 16], F32)
    n_qT = dscr('n_qT', [512, NTOK], BF16); n_kT = dscr('n_kT', [512, NTOK], BF16); n_v1 = dscr('n_v1', [NTOK, 520], BF16)
    l_xT = dscr('l_xT', [256, NTOK], BF16); l_gT = dscr('l_gT', [256, NTOK], BF16)
    mixT = dscr('mixT', [D, NTOK], BF16)

    with contextlib.ExitStack() as st:
        S = Sched(nc, st)

        uid = [0]

        def sb(stack, name, shape, dt):
            uid[0] += 1
            return stack.enter_context(nc.sbuf_tensor("sb%d_%s" % (uid[0], name), list(shape), dt))

        NPS = 7
        PS = [st.enter_context(nc.psum_tensor("ps%d" % i, [128, 512], F32)) for i in range(NPS)]
        PST = st.enter_context(nc.psum_tensor("pst", [128, 1024], BF16))
        psi = [0]

        def next_ps():
            i = psi[0]
            psi[0] = (i + 1) % NPS
            return PS[i], "ps%d" % i

        def mm(out, pairs, wname, rnames):
            n = len(pairs)
            for i, (l_, r_) in enumerate(pairs):
                f_ = (lambda e, l_=l_, r_=r_, i=i: e.matmul(out, lhsT=l_, rhs=r_, start=(i == 0), stop=(i == n - 1)))
                if i == n - 1:
                    S.op('pe', f_, reads=rnames, writes=[wname])
                else:
                    S.op_nosig('pe', f_, reads=rnames, writes=[wname])

        def act(out, in_, func, reads, writes, **kw):
            return S.op('act', lambda e: e.activation(out=out, in_=in_, func=func, **kw), reads=reads, writes=writes)

        def tt(out, in0, in1, op, reads, writes, eng='dve'):
            return S.op(eng, lambda e: e.tensor_tensor(out=out, in0=in0, in1=in1, op=op), reads=reads, writes=writes)

        def ts(out, in0, s1, op0, reads, writes, s2=None, op1=None, eng='dve'):
            if op1 is None:
                return S.op(eng, lambda e: e.tensor_scalar(out=out, in0=in0, scalar1=s1, scalar2=None, op0=op0), reads=reads, writes=writes)
            return S.op(eng, lambda e: e.tensor_scalar(out=out, in0=in0, scalar1=s1, scalar2=s2, op0=op0, op1=op1), reads=reads, writes=writes)

        def stt(out, in0, scalar, in1, op0, op1, reads, writes):
            return S.op('dve', lambda e: e.scalar_tensor_tensor(out=out, in0=in0, scalar=scalar, in1=in1, op0=op0, op1=op1),
                        reads=reads, writes=writes)

        def cp(out, in_, reads, writes, eng='dve'):
            return S.op(eng, lambda e: e.tensor_copy(out=out, in_=in_), reads=reads, writes=writes)

        constf = sb(st, 'constf', [128, 6, 128], F32)
        cb = sb(st, 'cb', [128, 6, 128], BF16)
        onesb = sb(st, 'onesb', [128, 128], BF16)
        c_one = sb(st, 'c_one', [128, 1], F32)
        c_eps = sb(st, 'c_eps', [128, 1], F32)
        modT = sb(st, 'modT', [128, L, 48, 2], F32)
        A1v = sb(st, 'A1v', [128, L, 2, 8], F32)
        A2v = sb(st, 'A2v', [128, L, 2, 8], F32)
        n1T = sb(st, 'n1T', [128, L, 8], F32); n2T = sb(st, 'n2T', [128, L, 8], F32)
        bmodT = sb(st, 'bmodT', [128, L, 48], F32)
        gateb = sb(st, 'gateb', [128, L, 16], F32)
        nqg = sb(st, 'nqg', [128, L], F32); nkg = sb(st, 'nkg', [128, L], F32)
        convw = sb(st, 'convw', [128, L, 2, 4], F32); convb = sb(st, 'convb', [128, L, 2], F32)
        lba = sb(st, 'lba', [128, L, 2, 2], F32); lbx = sb(st, 'lbx', [128, L, 2, 2], F32)
        llam = sb(st, 'llam', [128, L, 2, 2], F32)
        cvec = sb(st, 'cvec', [128, L, 2, 2], F32); cvec2 = sb(st, 'cvec2', [128, L, 2, 2], F32)
        cact = sb(st, 'cact', [128, 8, 2], F32)
        identf = constf[:, 0, :]
        identb = cb[:, 0, :]
        blockones = cb[:, 5, :]

        if stop_after is not None:
            touch = sb(st, 'touch', [1, 64], F32)
            S.op('dve', lambda e: e.memset(touch[:], 0.0), writes=['touch'])
            for ti, t_ in enumerate((xT0, w_in, w_mod, w_out, w1, w2, TB, lwa, lwx, ropec_d, ropes_d)):
                idx = tuple([slice(0, 1)] * (len(t_.shape) - 1) + [slice(0, 2)])
                src_ = t_[idx]
                while len(src_.shape) > 2:
                    src_ = src_.squeeze(0)
                S.dma('sp', touch[0:1, 2 * ti:2 * ti + 2], src_, writes=['touch'])
            S.dma('sp', outT[0:1, 0:64], touch[0:1, :], reads=['touch'])
        S.dma('sp', constf[:], consts_d[:, :, :], writes=['constf'])
        for tl, src, nm in ((n1T, n1T_d, 'n1T'), (n2T, n2T_d, 'n2T'), (bmodT, bmodT_d, 'bmodT'), (gateb, gateb_d, 'gateb'),
                            (convw, convw_d, 'convw'), (convb, convb_d, 'convb'),
                            (lba, lba_d, 'lba'), (lbx, lbx_d, 'lbx'), (llam, llam_d, 'llam'), (cact, cT, 'cact'),
                            (nqg, nqg_d, 'nqg'), (nkg, nkg_d, 'nkg')):
            S.dma('sp', tl[:], src, writes=[nm])
        import os
        DBGV = int(os.environ.get('DBGV', '9'))
        if DBGV >= 1:
            cp(cb[:], constf[:], ['constf'], ['cb'])
        if DBGV >= 2:
            S.op('dve', lambda e: e.memset(onesb[:], 1.0), writes=['onesb'])
            S.op('dve', lambda e: e.memset(c_one[:], 1.0), writes=['c_one'])
            S.op('dve', lambda e: e.memset(c_eps[:], EPS), writes=['c_eps'])
        if DBGV >= 3:
            act(cact[:], cact[:], AF.Silu, ['cact'], ['cact'])
        if DBGV >= 4:
            ts(nqg[:], nqg[:], 0.125, ALU.mult, ['nqg'], ['nqg'])
        if DBGV >= 5:
            act(cvec[:], llam[:], AF.Exp, ['llam'], ['cvec'], scale=-1.0)
        if DBGV >= 6:
            act(cvec[:], cvec[:], AF.Ln, ['cvec', 'c_one'], ['cvec'], bias=c_one[:, 0:1])
        if DBGV >= 7:
            ts(cvec2[:], cvec[:], -16.0, ALU.mult, ['cvec'], ['cvec2'])
            ts(cvec[:], cvec[:], -8.0, ALU.mult, ['cvec', 'cvec2'], ['cvec'])

        if stop_after == 'P0':
            S.flush()
            return nc

        def cast_weights(l, part='all'):
            kw = dict(max_dma_last_dim=2048)
            if part in ('all', 'in'):
                for k in range(8):
                    S.dma('pool', wb_in[l, k * 128:(k + 1) * 128, :], w_in[l, k * 128:(k + 1) * 128, :], **kw)
            if part in ('all', 'rest'):
                for k in range(8):
                    S.dma('pool', wb_out[l, k * 128:(k + 1) * 128, :], w_out[l, k * 128:(k + 1) * 128, :], **kw)
                for k in range(8):
                    S.dma('pool', wb_1[l, k * 128:(k + 1) * 128, :], w1[l, k * 128:(k + 1) * 128, :], **kw)
                for k in range(32):
                    S.dma('pool', wb_2[l, k * 128:(k + 1) * 128, :], w2[l, k * 128:(k + 1) * 128, :], **kw)

        if stop_after != 'P2only':
            cast_weights(0, 'in')
        if stop_after == 'P1':
            S.flush()
            return nc

        def mod_slab(l, i, wm, it):
            w_ = wm[it % 2]; wn = 'wm%d' % (it % 2)
            S.dma('sp', w_[:], w_mod[l, :, i * 512:(i + 1) * 512].rearrange("(k p) n -> p k n", p=128), writes=[wn])
            ps, pn = next_ps()
            for jj in range(4):
                mm(ps[:, jj * 2:jj * 2 + 2], [(w_[:, k, jj * 128:(jj + 1) * 128], cact[:, k, :]) for k in range(8)],
                   pn, [wn, 'cact'])
            for j in range(2):
                tt(modT[:, l, i * 4:(i + 1) * 4, j], ps[:, j:8:2], bmodT[:, l, i * 4:(i + 1) * 4], ALU.add,
                   [pn, 'bmodT'], ['modT%d' % l])

        def mod_finish(l):
            for j in range(2):
                stt(A1v[:, l, j, :], modT[:, l, 8:16, j], 1.0, n1T[:, l, :], ALU.add, ALU.mult, ['modT%d' % l, 'n1T'], ['A1v'])
                stt(A2v[:, l, j, :], modT[:, l, 32:40, j], 1.0, n2T[:, l, :], ALU.add, ALU.mult, ['modT%d' % l, 'n2T'], ['A2v'])

        with contextlib.ExitStack() as ph:
            wm = [sb(ph, 'wm%d' % i, [128, 8, 512], F32) for i in range(2)]
            for i in range(12):
                mod_slab(0, i, wm, i)
            mod_finish(0)
            S.flush()
        if stop_after in ('P2', 'P2only'):
            return nc
        def phase_A(l):
            xsrc = xT0 if l == 0 else xT
            with contextlib.ExitStack() as ph:
                W = sb(ph, 'wA', [128, 8, NCOL], BF16)
                xts = [sb(ph, 'xtA%d' % i, [128, 8, 512], F32) for i in range(2)]
                sq = sb(ph, 'sqA', [128, 8, 512], BF16)
                hTs = [sb(ph, 'hTA%d' % i, [128, 8, 512], BF16) for i in range(2)]
                lnv = sb(ph, 'lnvA', [128, 512], F32)
                rstd = sb(ph, 'rstdA', [128, 512], F32)
                t1 = [sb(ph, 't1A%d' % i, [128, 512], F32) for i in range(2)]
                nsq = [sb(ph, 'nsq%d' % i, [128, 512], BF16) for i in range(2)]
                nln = [sb(ph, 'nln%d' % i, [128, 512], F32) for i in range(2)]
                nrs = [sb(ph, 'nrs%d' % i, [128, 512], F32) for i in range(2)]
                mqk_st = sb(ph, 'mqk_st', [128, 4, 512], BF16)
                mv_st = sb(ph, 'mv_st', [128, 4, 4, 65], BF16)
                mo_st = sb(ph, 'mo_st', [128, 4, 256], BF16)
                mg_st = sb(ph, 'mg_st', [128, 4, 16], F32)
                nv_st = sb(ph, 'nv_st', [128, 4, 8, 65], BF16)
                nq_st = sb(ph, 'nq_st', [128, 4, 512], BF16)
                nk_st = sb(ph, 'nk_st', [128, 4, 512], BF16)
                lx_st = sb(ph, 'lx_st', [128, 2, 512], BF16)
                lg_st = sb(ph, 'lg_st', [128, 2, 512], BF16)
                slabs = [(0, 256, 1)] + [(256 + 512 * i, 512, 0) for i in range(8)]
                RCa = [sb(ph, 'aRC%d' % i, [128, 4, 64], F32) for i in range(2)]
                RSa = [sb(ph, 'aRS%d' % i, [128, 4, 64], F32) for i in range(2)]
                rt1 = sb(ph, 'art1', [128, 512], F32)
                rt2 = sb(ph, 'art2', [128, 512], F32)

                def xbuf(si):
                    return xts[si % 2], 'xtA%d' % (si % 2)

                def hbuf(si):
                    return hTs[si % 2], 'hTA%d' % (si % 2)

                def norm1(si):
                    t0, T, j = slabs[si]
                    xt, xn = xbuf(si)
                    S.dma('sp', xt[:, :, 0:T], xsrc[:, t0:t0 + T].rearrange("(k p) t -> p k t", p=128), writes=[xn])
                    if si >= 1:
                        S.dma('sp', RCa[si % 2][:], ropec_d[:, 4 * (si - 1):4 * si, :], writes=['aRC%d' % (si % 2)])
                        S.dma('sp', RSa[si % 2][:], ropes_d[:, 4 * (si - 1):4 * si, :], writes=['aRS%d' % (si % 2)])
                    act(sq[:, :, 0:T], xt[:, :, 0:T], AF.Square, [xn], ['sqA'])

                ones_ps = {}

                def norm2(si):
                    t0, T, j = slabs[si]
                    ps, pn = next_ps()
                    mm(ps[:, 0:T], [(onesb[:, :], sq[:, k, 0:T]) for k in range(8)], pn, ['sqA', 'onesb'])
                    act(lnv[:, 0:T], ps[:, 0:T], AF.Ln, [pn, 'c_eps'], ['lnvA'], scale=1.0 / D, bias=c_eps[:, 0:1])

                def norm3(si):
                    t0, T, j = slabs[si]
                    xt, xn = xbuf(si)
                    hT, hn = hbuf(si)
                    act(rstd[:, 0:T], lnv[:, 0:T], AF.Exp, ['lnvA'], ['rstdA'], scale=-0.5)
                    for k in range(8):
                        tb_ = t1[k % 2]; tn = 't1A%d' % (k % 2)
                        stt(tb_[:, 0:T], xt[:, k, 0:T], A1v[:, l, j, k:k + 1], rstd[:, 0:T], ALU.mult, ALU.mult,
                            [xn, 'rstdA', 'A1v'], [tn])
                        act(hT[:, k, 0:T], tb_[:, 0:T], AF.Identity, [tn, 'modT'], [hn], bias=modT[:, l, k, j:j + 1], scale=1.0)

                nnc = [0]

                def groups(si):
                    t0, T, j = slabs[si]
                    hT, hn = hbuf(si)
                    G = []

                    def fm(col, M=128):
                        ps_, pn_ = next_ps()
                        mm(ps_[0:M, 0:T], [(W[:, k, col:col + M], hT[:, k, 0:T]) for k in range(8)], pn_, [hn, 'wA'])
                        return ps_, pn_

                    held = {}

                    def g_nqk1(key, c0, i):
                        b_ = nnc[0] % 2; nnc[0] += 1
                        ps_, pn_ = fm(c0 + 128 * i)
                        act(nsq[b_][:, 0:T], ps_[:, 0:T], AF.Square, [pn_], ['nsq%d' % b_])
                        held[key] = (b_, ps_, pn_)

                    def g_nqk2(key, gvec, gname, stg, sname, i):
                        b_, ps_, pn_ = held.pop(key)
                        ps2, pn2 = next_ps()
                        mm(ps2[:, 0:T], [(blockones, nsq[b_][:, 0:T])], pn2, ['nsq%d' % b_, 'cb'])
                        act(nln[b_][:, 0:T], ps2[:, 0:T], AF.Ln, [pn2, 'c_eps'], ['nln%d' % b_], scale=1.0 / HD, bias=c_eps[:, 0:1])
                        act(nrs[b_][:, 0:T], nln[b_][:, 0:T], AF.Exp, ['nln%d' % b_], ['nrs%d' % b_], scale=-0.5)
                        stt(stg[:, i, 0:T], ps_[:, 0:T], gvec[:, l:l + 1], nrs[b_][:, 0:T], ALU.mult, ALU.mult,
                            [pn_, 'nrs%d' % b_, gname], [sname])
                    p1s, p2s = [], []
                    for (c0, gvec, gname, stg, sname) in ((C_NQ, nqg, 'nqg', nq_st, 'nq_st'), (C_NK, nkg, 'nkg', nk_st, 'nk_st')):
                        for i in range(4):
                            key = (c0, i)
                            p1s.append(lambda key=key, c0=c0, i=i: g_nqk1(key, c0, i))
                            p2s.append(lambda key=key, gvec=gvec, gname=gname, stg=stg, sname=sname, i=i: g_nqk2(key, gvec, gname, stg, sname, i))
                    G.append(p1s[0])
                    for gi in range(8):
                        if gi + 1 < 8:
                            G.append(p1s[gi + 1])
                        G.append(p2s[gi])

                    def g_lx(i):
                        ps_, pn_ = fm(C_LX + 128 * i)
                        act(lx_st[:, i, 0:T], ps_[:, 0:T], AF.Copy, [pn_], ['lx_st'])

                    def g_lg(i):
                        ps_, pn_ = fm(C_LG + 128 * i)
                        act(lg_st[:, i, 0:T], ps_[:, 0:T], AF.Gelu, [pn_], ['lg_st'])
                    for i in range(2):
                        G.append(lambda i=i: g_lx(i))
                    for i in range(2):
                        G.append(lambda i=i: g_lg(i))

                    def g_tok(s, kind, si=si):
                        tsl = slice(s * 128, (s + 1) * 128)
                        ps_, pn_ = next_ps()
                        if kind == 0:
                            mm(ps_[:, :], [(hT[:, k, tsl], W[:, k, C_MQ:C_MQ + 512]) for k in range(8)], pn_, [hn, 'wA'])
                            if si == 0:
                                cp(mqk_st[:, s, :], ps_[:, :], [pn_], ['mqk_st'])
                            else:
                                rc = RCa[si % 2]; rs = RSa[si % 2]
                                rcn = 'aRC%d' % (si % 2); rsn_ = 'aRS%d' % (si % 2)
                                tt(rt1[:].rearrange("p (h d) -> p h d", d=64), ps_[:, :].rearrange("p (h d) -> p h d", d=64),
                                   bc(rc[:, s, :], 0, 8), ALU.mult, [pn_, rcn], ['art1'])
                                qv = ps_[:, :].rearrange("p (h f a e) -> p h f a e", h=8, f=2, a=2, e=16)
                                tv = rt2[:].rearrange("p (h f a e) -> p h f a e", h=8, f=2, a=2, e=16)
                                rv = rs[:, s, :].rearrange("p (f a e) -> p f a e", f=2, a=2, e=16)
                                for a_ in range(2):
                                    tt(tv[:, :, :, a_, :], qv[:, :, :, 1 - a_, :], bc(rv[:, :, a_, :], 0, 8), ALU.mult, [pn_, rsn_], ['art2'])
                                tt(mqk_st[:, s, :], rt1[:], rt2[:], ALU.add, ['art1', 'art2'], ['mqk_st'])
                        elif kind == 1:
                            mm(ps_[:, :], [(hT[:, k, tsl], W[:, k, C_MV:C_MV + 512]) for k in range(8)], pn_, [hn, 'wA'])
                            cp(mv_st[:, s, :, 0:64], ps_[:, 0:256].rearrange("p (h d) -> p h d", d=64), [pn_], ['mv_st'])
                            act(mo_st[:, s, :], ps_[:, 256:512], AF.Sigmoid, [pn_], ['mo_st'])
                        elif kind == 2:
                            mm(ps_[:, :], [(hT[:, k, tsl], W[:, k, C_NV:C_NV + 512]) for k in range(8)], pn_, [hn, 'wA'])
                            cp(nv_st[:, s, :, 0:64], ps_[:, :].rearrange("p (h d) -> p h d", d=64), [pn_], ['nv_st'])
                        else:
                            mm(ps_[:, 0:16], [(hT[:, k, tsl], W[:, k, C_GI:C_GI + 16]) for k in range(8)], pn_, [hn, 'wA'])
                            tt(mg_st[:, s, :], ps_[:, 0:16], gateb[:, l, :], ALU.add, [pn_, 'gateb'], ['mg_st'])
                    for s in range(T // 128):
                        for kind in range(4):
                            G.append(lambda s=s, kind=kind: g_tok(s, kind))
                    return G

                def stores(si):
                    t0, T, j = slabs[si]
                    nsub = T // 128
                    tok = slice(t0, t0 + T)

                    def tokmaj(dst):
                        return dst[tok, :].rearrange("(s p) c -> p s c", p=128)
                    S.dma('pool', tokmaj(m_q), mqk_st[:, 0:nsub, 0:256], reads=['mqk_st'])
                    S.dma('pool', tokmaj(m_k), mqk_st[:, 0:nsub, 256:512], reads=['mqk_st'])
                    S.dma('pool', tokmaj(m_v1), mv_st[:, 0:nsub].rearrange("p s h e -> p s (h e)"), reads=['mv_st'])
                    S.dma('pool', tokmaj(m_o), mo_st[:, 0:nsub, :], reads=['mo_st'])
                    S.dma('pool', tokmaj(m_g), mg_st[:, 0:nsub, :], reads=['mg_st'])
                    S.dma('pool', tokmaj(n_v1), nv_st[:, 0:nsub].rearrange("p s h e -> p s (h e)"), reads=['nv_st'])
                    S.dma('pool', n_qT[:, tok].rearrange("(i p) t -> p i t", p=128), nq_st[:, :, 0:T], reads=['nq_st'])
                    S.dma('pool', n_kT[:, tok].rearrange("(i p) t -> p i t", p=128), nk_st[:, :, 0:T], reads=['nk_st'])
                    S.dma('pool', l_xT[:, tok].rearrange("(i p) t -> p i t", p=128), lx_st[:, :, 0:T], reads=['lx_st'])
                    S.dma('pool', l_gT[:, tok].rearrange("(i p) t -> p i t", p=128), lg_st[:, :, 0:T], reads=['lg_st'])

                norm1(0)
                for k in range(8):
                    S.dma('sp', W[:, k, :], wb_in[l, k * 128:(k + 1) * 128, :], writes=['wA'])
                S.op('dve', lambda e: e.memset(mv_st[:, :, :, 64:65], 1.0), writes=['mv_st'])
                S.op('dve', lambda e: e.memset(nv_st[:, :, :, 64:65], 1.0), writes=['nv_st'])
                norm2(0)
                norm3(0)
                for si in range(len(slabs)):
                    G = groups(si)
                    n = len(G)
                    nxt = si + 1 if si + 1 < len(slabs) else None
                    a, b = n // 4, n // 2
                    if nxt is not None:
                        norm1(nxt)
                    for g_ in G[:a]:
                        g_()
                    if nxt is not None:
                        norm2(nxt)
                    for g_ in G[a:b]:
                        g_()
                    if nxt is not None:
                        norm3(nxt)
                    for g_ in G[b:]:
                        g_()
                    stores(si)
                S.flush()

        SLABS9 = [(0, 256)] + [(256 + 512 * i, 512) for i in range(8)]

        def scan(out, d0, d1, init, reads, writes):
            return S.op('dve', lambda e: e.tensor_tensor_scan(out=out, data0=d0, data1=d1, initial=init, op0=ALU.mult, op1=ALU.add),
                        reads=reads, writes=writes)

        def phase_lru(l):
            with contextlib.ExitStack() as ph:
                X = sb(ph, 'lX', [128, 2, NTOK], BF16)
                G = sb(ph, 'lG', [128, 2, NTOK], BF16)
                XL = sb(ph, 'lXL', [128, NTOK], F32)
                XLb = sb(ph, 'lXLb', [128, NTOK], BF16)
                R = sb(ph, 'lR', [128, NTOK], F32)
                I_ = sb(ph, 'lI', [128, NTOK], F32)
                Bt = sb(ph, 'lBt', [128, NTOK], F32)
                HF = sb(ph, 'lHF', [128, NTOK], F32)
                HB = sb(ph, 'lHB', [128, NTOK], F32)
                Y = sb(ph, 'lY', [128, NTOK], BF16)
                WBf = sb(ph, 'lWBf', [128, 2, 2, 2, 128], F32)
                WB = sb(ph, 'lWB', [128, 2, 2, 2, 128], BF16)
                do_mod = (l + 1 < L)
                if do_mod:
                    wm = [sb(ph, 'lwm%d' % i, [128, 8, 512], F32) for i in range(2)]
                mod_i = [0]

                def mod_step():
                    if do_mod and mod_i[0] < 12:
                        mod_slab(l + 1, mod_i[0], wm, mod_i[0])
                        mod_i[0] += 1
                for i in range(2):
                    S.dma('sp', X[:, i, :], l_xT[i * 128:(i + 1) * 128, :], writes=['lX'])
                if l == 0:
                    cast_weights(0, 'rest')
                S.op('dve', lambda e: e.memset(WBf[:], 0.0), writes=['lWBf'])
                for d in range(2):
                    for ct in range(2):
                        for b in range(2):
                            for ax, src in ((0, lwa), (1, lwx)):
                                S.dma('sp', WBf[b * 64:(b + 1) * 64, d, ct, ax, b * 64:(b + 1) * 64], src[l, d, 2 * ct + b],
                                      reads=['lWBf'], writes=['lWBf%d%d%d%d' % (d, ct, b, ax)])
                allw = ['lWBf'] + ['lWBf%d%d%d%d' % (d, ct, b, ax) for d in range(2) for ct in range(2) for b in range(2) for ax in range(2)]
                cp(WB[:], WBf[:], allw, ['lWB'])
                for i in range(2):
                    S.dma('sp', G[:, i, :], l_gT[i * 128:(i + 1) * 128, :], writes=['lG'])
                for ct in range(2):
                    act(XL[:], X[:, ct, :], AF.Identity, ['lX', 'convw', 'convb'], ['lXL'],
                        scale=convw[:, l, ct, 2:3], bias=convb[:, l, ct:ct + 1])
                    for (s0, s1) in ((0, NCTX), (NCTX, NTOK)):
                        stt(XL[:, s0 + 2:s1], X[:, ct, s0:s1 - 2], convw[:, l, ct, 0:1], XL[:, s0 + 2:s1], ALU.mult, ALU.add,
                            ['lX', 'lXL', 'convw'], ['lXL'])
                        stt(XL[:, s0 + 1:s1], X[:, ct, s0:s1 - 1], convw[:, l, ct, 1:2], XL[:, s0 + 1:s1], ALU.mult, ALU.add,
                            ['lX', 'lXL', 'convw'], ['lXL'])
                        stt(XL[:, s0:s1 - 1], X[:, ct, s0 + 1:s1], convw[:, l, ct, 3:4], XL[:, s0:s1 - 1], ALU.mult, ALU.add,
                            ['lX', 'lXL', 'convw'], ['lXL'])
                    act(XLb[:], XL[:], AF.Copy, ['lXL'], ['lXLb'])
                    for d in range(2):
                        for (t0, T) in SLABS9:
                            ps, pn = next_ps()
                            mm(ps[:, 0:T], [(WB[:, d, ct, 0, :], XLb[:, t0:t0 + T])], pn, ['lXLb', 'lWB'])
                            act(R[:, t0:t0 + T], ps[:, 0:T], AF.Sigmoid, [pn, 'lba'], ['lR'], bias=lba[:, l, d, ct:ct + 1])
                            ps, pn = next_ps()
                            mm(ps[:, 0:T], [(WB[:, d, ct, 1, :], XLb[:, t0:t0 + T])], pn, ['lXLb', 'lWB'])
                            act(I_[:, t0:t0 + T], ps[:, 0:T], AF.Sigmoid, [pn, 'lbx'], ['lI'], bias=lbx[:, l, d, ct:ct + 1])
                            if t0 in (256, 1792, 3328):
                                mod_step()
                        act(Bt[:], R[:], AF.Exp, ['lR', 'cvec2'], ['lBt'], scale=cvec2[:, l, d, ct:ct + 1])
                        act(R[:], R[:], AF.Exp, ['lR', 'cvec'], ['lR'], scale=cvec[:, l, d, ct:ct + 1])
                        ts(Bt[:], Bt[:], -1.0, ALU.mult, ['lBt'], ['lBt'], s2=1.0, op1=ALU.add)
                        act(Bt[:], Bt[:], AF.Sqrt, ['lBt'], ['lBt'])
                        tt(Bt[:], Bt[:], I_[:], ALU.mult, ['lBt', 'lI'], ['lBt'])
                        tt(Bt[:], Bt[:], XL[:], ALU.mult, ['lBt', 'lXL'], ['lBt'])
                        if d == 0:
                            scan(HF[:], R[:], Bt[:], 0.0, ['lR', 'lBt'], ['lHF'])
                        else:
                            scan(HB[:, 0:NCTX][:, ::-1], R[:, 0:NCTX][:, ::-1], Bt[:, 0:NCTX][:, ::-1], 0.0, ['lR', 'lBt'], ['lHB'])
                            scan(HB[:, NCTX:NTOK][:, ::-1], R[:, NCTX:NTOK][:, ::-1], Bt[:, NCTX:NTOK][:, ::-1], HB[:, 0:1],
                                 ['lR', 'lBt', 'lHB'], ['lHB'])
                    tt(HF[:], HF[:], HB[:], ALU.add, ['lHF', 'lHB'], ['lHF'])
                    tt(Y[:], HF[:], G[:, ct, :], ALU.mult, ['lHF', 'lG'], ['lY'])
                    S.dma('pool', mixT[768 + 128 * ct:768 + 128 * (ct + 1), :], Y[:], reads=['lY'])
                while do_mod and mod_i[0] < 12:
                    mod_step()
                if do_mod:
                    mod_finish(l + 1)
                S.flush()

        def phase_C(l):
            last = (l == L - 1)
            xsrc = xT0 if l == 0 else xT
            with contextlib.ExitStack() as ph:
                Wo = sb(ph, 'cWo', [128, 8, D], BF16)
                W1 = sb(ph, 'cW1', [128, 8, MLPH], BF16)
                W2 = sb(ph, 'cW2', [128, 32, D], BF16)
                xts = [sb(ph, 'cxt%d' % i, [128, 8, 256], F32) for i in range(2)]
                mx = sb(ph, 'cmx', [128, 8, 256], BF16)
                sq = sb(ph, 'csq', [128, 8, 256], BF16)
                h2 = sb(ph, 'ch2', [128, 8, 256], BF16)
                hid = sb(ph, 'chid', [128, 32, 256], BF16)
                lnv = sb(ph, 'clnv', [128, 256], F32)
                rstd = sb(ph, 'crstd', [128, 256], F32)
                t1 = [sb(ph, 'ct1%d' % i, [128, 256], F32) for i in range(2)]
                rl = [sb(ph, 'crl%d' % i, [128, 256], F32) for i in range(2)]
                for k in range(8):
                    S.dma('sp', Wo[:, k, :], wb_out[l, k * 128:(k + 1) * 128, :], writes=['cWo'])
                T = 256
                tiles = [ti for ti in range(NTOK // T) if not (last and ti == 0)]

                def X(ti):
                    return xts[ti % 2], 'cxt%d' % (ti % 2)

                def load(ti):
                    xt, xn = X(ti)
                    tok = slice(ti * T, (ti + 1) * T)
                    S.dma('sp', mx[:], mixT[:, tok].rearrange("(k p) t -> p k t", p=128), writes=['cmx'])
                    S.dma('sp', xt[:], xsrc[:, tok].rearrange("(k p) t -> p k t", p=128), writes=[xn])

                def outproj(ti):
                    xt, xn = X(ti)
                    j = 1 if ti == 0 else 0
                    for m in range(8):
                        ps, pn = next_ps()
                        mm(ps[:, 0:T], [(Wo[:, k, m * 128:(m + 1) * 128], mx[:, k, :]) for k in range(8)], pn, ['cmx', 'cWo'])
                        stt(xt[:, m, :], ps[:, 0:T], modT[:, l, 16 + m, j:j + 1], xt[:, m, :], ALU.mult, ALU.add,
                            [pn, xn, 'modT'], [xn])

                ones_ps = {}

                def square(ti):
                    xt, xn = X(ti)
                    act(sq[:], xt[:], AF.Square, [xn], ['csq'])

                def onesmm(ti):
                    ps, pn = next_ps()
                    mm(ps[:, 0:T], [(onesb[:, :], sq[:, k, :]) for k in range(8)], pn, ['csq', 'onesb'])
                    ones_ps[ti] = (ps, pn)

                def normrest(ti):
                    xt, xn = X(ti)
                    j = 1 if ti == 0 else 0
                    ps, pn = ones_ps.pop(ti)
                    act(lnv[:], ps[:, 0:T], AF.Ln, [pn, 'c_eps'], ['clnv'], scale=1.0 / D, bias=c_eps[:, 0:1])
                    act(rstd[:], lnv[:], AF.Exp, ['clnv'], ['crstd'], scale=-0.5)
                    for k in range(8):
                        tb_ = t1[k % 2]; tn = 'ct1%d' % (k % 2)
                        stt(tb_[:], xt[:, k, :], A2v[:, l, j, k:k + 1], rstd[:], ALU.mult, ALU.mult, [xn, 'crstd', 'A2v'], [tn])
                        act(h2[:, k, :], tb_[:], AF.Identity, [tn, 'modT'], ['ch2'], bias=modT[:, l, 24 + k, j:j + 1], scale=1.0)

                def up(ti):
                    for m in range(32):
                        ps, pn = next_ps()
                        mm(ps[:, 0:T], [(W1[:, k, m * 128:(m + 1) * 128], h2[:, k, :]) for k in range(8)], pn, ['ch2', 'cW1'])
                        rb = rl[m % 2]; rn = 'crl%d' % (m % 2)
                        act(rb[:], ps[:, 0:T], AF.Relu, [pn], [rn])
                        tt(hid[:, m, :], rb[:], rb[:], ALU.mult, [rn], ['chid'])

                def down(ti, half):
                    xt, xn = X(ti)
                    j = 1 if ti == 0 else 0
                    for m in range(4 * half, 4 * half + 4):
                        ps, pn = next_ps()
                        mm(ps[:, 0:T], [(W2[:, k, m * 128:(m + 1) * 128], hid[:, k, :]) for k in range(32)], pn, ['chid', 'cW2'])
                        stt(xt[:, m, :], ps[:, 0:T], modT[:, l, 40 + m, j:j + 1], xt[:, m, :], ALU.mult, ALU.add,
                            [pn, xn, 'modT'], [xn])

                def store(ti):
                    xt, xn = X(ti)
                    if last:
                        dst = outT[:, ti * T - NCTX:(ti + 1) * T - NCTX]
                    else:
                        dst = xT[:, ti * T:(ti + 1) * T]
                    S.dma('sp', dst.rearrange("(k p) t -> p k t", p=128), xt[:], reads=[xn])

                load(tiles[0])
                for k in range(8):
                    S.dma('sp', W1[:, k, :], wb_1[l, k * 128:(k + 1) * 128, :], writes=['cW1'])
                for g_ in range(4):
                    S.dma('sp', W2[:, 8 * g_:8 * g_ + 8, :], wb_2[l, 1024 * g_:1024 * (g_ + 1), :].rearrange("(k p) n -> p k n", p=128),
                          writes=['cW2'])
                if l + 1 < L:
                    cast_weights(l + 1)
                prev = None
                for ti in tiles:
                    if prev is not None:
                        load(ti)
                    outproj(ti)
                    square(ti)
                    if prev is not None:
                        down(prev, 0)
                    onesmm(ti)
                    normrest(ti)
                    if prev is not None:
                        down(prev, 1)
                        store(prev)
                    up(ti)
                    prev = ti
                down(prev, 0)
                down(prev, 1)
                store(prev)
                S.flush()

        def phase_na(l):
            want_ctx = l < L - 1
            with contextlib.ExitStack() as ph:
                Q = sb(ph, 'nQ', [128, 4, NTOK], BF16)
                Kt = sb(ph, 'nK', [128, 4, NTOK], BF16)
                V = sb(ph, 'nV', [128, NCH, 520], BF16)
                BT = sb(ph, 'nBT', [128, nslots, 8, 128], BF16)
                PT = [sb(ph, 'nPT%d' % i, [128, 7 * 128], BF16) for i in range(3)]
                Yst = [sb(ph, 'nY%d' % i, [128, 512], BF16) for i in range(2)]
                rr = [sb(ph, 'nrr%d' % i, [128, 4], F32) for i in range(2)]
                MIXs = [sb(ph, 'nMIX%d' % i, [128, 4, 512], BF16) for i in range(2)]
                Qz = [sb(ph, 'nQz%d' % i, [128, 8, 128], BF16) for i in range(2)]
                for i in range(2):
                    S.op('pool', lambda e, i=i: e.memset(Qz[i][:], 0.0), writes=['nQz%d' % i])
                for i in range(4):
                    S.dma('sp', Q[:, i, :], n_qT[i * 128:(i + 1) * 128, :], writes=['nQ%d' % i])
                S.dma('sp', Kt[:, 0, :], n_kT[0:128, :], writes=['nK0'])
                S.dma('sp', V[:, 0:17, :], n_v1[0:17 * 128, :].rearrange("(c p) e -> p c e", p=128), writes=['nV0'])
                for i in range(1, 4):
                    S.dma('sp', Kt[:, i, :], n_kT[i * 128:(i + 1) * 128, :], writes=['nK%d' % i])
                S.dma('sp', V[:, 17:34, :], n_v1[17 * 128:34 * 128, :].rearrange("(c p) e -> p c e", p=128), writes=['nV1'])
                BTs = [sb(ph, 'nBTs%d' % i, [128, 8, 128], F32) for i in range(2)]
                btw = []
                si = 0
                for sig, slot0 in cases.items():
                    for i, (dk, blocks) in enumerate(sig):
                        stg = BTs[si % 2]; sn = 'nBTs%d' % (si % 2); si += 1
                        S.op('dve', lambda e, stg=stg: e.memset(stg[:], NEG), writes=[sn])
                        names = []
                        for bi, dr in enumerate(blocks):
                            if dr is None:
                                continue
                            b, a = bi // 2, bi % 2
                            nm = sn + '_%d' % bi
                            S.dma('sp', stg[b * 64:(b + 1) * 64, :, a * 64:(a + 1) * 64], TB[l, :, dr, :, :], reads=[sn], writes=[nm])
                            names.append(nm)
                        nm2 = 'nBT_%d' % (slot0 + i)
                        if si % 2 == 0:
                            act(BT[:, slot0 + i], stg[:], AF.Copy, [sn] + names, [nm2])
                        else:
                            cp(BT[:, slot0 + i], stg[:], [sn] + names, [nm2])
                        S.res[sn][1].append(S.res[nm2][0])
                        btw.append(nm2)
                qtiles = [(2 + j, plans[j]) for j in range(32)]
                if want_ctx:
                    qtiles = [(0, None), (1, None)] + qtiles
                units = []
                for qi, (tq, plan) in enumerate(qtiles):
                    if plan is None:
                        tiles = [(0, None), (1, None)]
                    else:
                        kts, sig = plan
                        slot0 = cases[sig]
                        tiles = [(2 + kt, slot0 + i) for i, kt in enumerate(kts)] + [(0, None), (1, None)]
                    for h in range(8):
                        units.append((qi, tq, tiles, h))
                state = {}

                def emit_qk(u):
                    qi, tq, tiles, h = units[u]
                    qsl = slice(tq * 128, (tq + 1) * 128)
                    qz = Qz[qi % 2]; qzn = 'nQz%d' % (qi % 2)
                    if h == 0:
                        cp(qz[0:64, 0:8:2, :], Q[0:64, :, qsl], ['nQ0', 'nQ1', 'nQ2', 'nQ3'], [qzn], eng='pool')
                        cp(qz[64:128, 1:8:2, :], Q[64:128, :, qsl], ['nQ0', 'nQ1', 'nQ2', 'nQ3'], [qzn], eng='pool')
                    n = len(tiles)
                    hp = h // 2
                    psA, pnA = next_ps()
                    psB, pnB = (next_ps() if n > 4 else (None, None))
                    for i, (tk, slot) in enumerate(tiles):
                        reg = (psA if i < 4 else psB)[:, (i % 4) * 128:(i % 4 + 1) * 128]
                        pn = pnA if i < 4 else pnB
                        pairs = [(Kt[:, hp, tk * 128:(tk + 1) * 128], qz[:, h, :])]
                        if slot is not None:
                            pairs.append((identb, BT[:, slot, h, :]))
                        mm(reg, pairs, pn, ['nK%d' % hp, qzn, 'cb'] + (['nBT_%d' % slot] if slot is not None else []))
                    state[u] = (psA, pnA, psB, pnB)

                def emit_rest(u):
                    qi, tq, tiles, h = units[u]
                    psA, pnA, psB, pnB = state.pop(u)
                    n = len(tiles)
                    nA = min(n, 4)
                    yb = Yst[qi % 2]; yn = 'nY%d' % (qi % 2)
                    pb = PT[u % 3]; pnm = 'nPT%d' % (u % 3)
                    act(pb[:, 0:nA * 128], psA[:, 0:nA * 128], AF.Exp, [pnA], [pnm + 'a'])
                    if n > 4:
                        act(pb[:, 512:n * 128], psB[:, 0:(n - 4) * 128], AF.Exp, [pnB], [pnm + 'b'])
                    hq = h % 4
                    if hq == 0:
                        state['psO'] = next_ps()
                    psO, pnO = state['psO']
                    mm(psO[:, hq * 65:(hq + 1) * 65],
                       [(pb[:, i * 128:(i + 1) * 128], V[:, tk, h * 65:(h + 1) * 65]) for i, (tk, _) in enumerate(tiles)],
                       pnO, [pnm + 'a', pnm + 'b', 'nV0', 'nV1'])
                    if hq == 3:
                        half = h // 4
                        rb = rr[half]; rn = 'nrr%d' % half
                        o3 = psO[:, 0:260].rearrange("p (h e) -> p h e", e=65)
                        S.op('dve', lambda e, rb=rb, o3=o3: e.reciprocal(out=rb[:], in_=o3[:, :, 64]), reads=[pnO], writes=[rn])
                        tt(yb[:, half * 256:(half + 1) * 256].rearrange("p (h d) -> p h d", d=64), o3[:, :, 0:64],
                           bc(rb[:], 1, 64), ALU.mult, [pnO, rn], [yn])
                    if h == 7:
                        for i in range(4):
                            S.op('pe', lambda e, i=i, yb=yb: e.transpose(out=PST[:, i * 128:(i + 1) * 128], in_=yb[:, i * 128:(i + 1) * 128], identity=identb),
                                 reads=[yn, 'cb'], writes=['pst'])
                        mb = MIXs[(tq // 4) % 2]; mn = 'nMIX%d' % ((tq // 4) % 2)
                        cp(mb[:, :, (tq % 4) * 128:(tq % 4 + 1) * 128], PST[:, 0:512].rearrange("p (i t) -> p i t", t=128), ['pst'], [mn])
                        grp_last = (tq % 4 == 3) or (tq == 1) or (tq == NCH - 1)
                        if grp_last:
                            g0 = (tq // 4) * 4
                            g1 = tq + 1
                            if tq == 1:
                                S.dma('sp', mixT[256:768, 0:256].rearrange("(i p) t -> p i t", p=128), mb[:, :, 0:256], reads=[mn])
                            elif g0 == 0:
                                S.dma('sp', mixT[256:768, 256:512].rearrange("(i p) t -> p i t", p=128), mb[:, :, 256:512], reads=[mn])
                            else:
                                S.dma('sp', mixT[256:768, g0 * 128:g1 * 128].rearrange("(i p) t -> p i t", p=128),
                                      mb[:, :, 0:(g1 - g0) * 128], reads=[mn])

                emit_qk(0)
                emit_qk(1)
                for u in range(len(units)):
                    if u + 2 < len(units):
                        emit_qk(u + 2)
                    emit_rest(u)
                S.flush()


        def phase_mlstm(l):
            import math
            with contextlib.ExitStack() as ph:
                QK = sb(ph, 'mQK', [128, NCH, 512], BF16)
                V1 = sb(ph, 'mV1', [128, NCH, 260], BF16)
                G16 = sb(ph, 'mG', [128, NCH, 16], F32)
                RCs = [sb(ph, 'mRC%d' % i, [128, 2, 64], F32) for i in range(2)]
                RSs = [sb(ph, 'mRS%d' % i, [128, 2, 64], F32) for i in range(2)]
                SP = sb(ph, 'mSP', [128, NCH, 8], F32)
                EQ = sb(ph, 'mEQ', [128, 2, NCH, 4], F32)
                EK = sb(ph, 'mEK', [128, 2, NCH, 4], F32)
                TMPG = sb(ph, 'mTG', [128, NCH, 4], F32)
                Dd = sb(ph, 'mD', [128, 2, 2, NCH], F32)
                Dfull = sb(ph, 'mDf', [128, 65, NCH], F32)
                Cst = sb(ph, 'mC', [128, 2, 65, NCH], F32)
                CB = sb(ph, 'mCB', [128, 4, NCH, 65], BF16)
                QsT = sb(ph, 'mQsT', [128, 2, NTOK], BF16)
                KsT = sb(ph, 'mKsT', [128, 2, NTOK], BF16)
                Qs = sb(ph, 'mQs', [128, 4, 256], BF16)
                Ks = sb(ph, 'mKs', [128, 4, 256], BF16)
                t1 = sb(ph, 'mt1', [128, 2, 512], F32)
                t2 = sb(ph, 'mt2', [128, 512], F32)
                SpT = [sb(ph, 'mSpT%d' % i, [128, 4, 128], BF16) for i in range(2)]
                HFs = sb(ph, 'mHF', [128, NCH, 256], BF16)
                SOs = [sb(ph, 'mSO%d' % i, [128, 4, 256], BF16) for i in range(2)]
                def two(nm, shp, dt):
                    return [sb(ph, nm + str(i), shp, dt) for i in range(2)]
                dd2 = two('mdd', [128, 4], F32); rr2 = two('mrr', [128, 4], F32)
                hb2 = two('mhb', [128, 256], F32); hs2 = two('mhs', [128, 256], F32); sq22 = two('msq2', [128, 256], F32)
                ss2 = two('mss', [128, 4], F32); lnn2 = two('mlnn', [128, 4], F32); rsn2 = two('mrsn', [128, 4], F32)
                y12 = two('my1', [128, 256], F32); YM2 = two('mYM', [128, 256], BF16)
                YMB = two('mYMB', [128, 4, 256], BF16)
                ssB = two('mssB', [128, 16], F32); lnB = two('mlnB', [128, 16], F32); rsB = two('mrsB', [128, 16], F32)
                MIXs = [sb(ph, 'mMIX%d' % i, [128, 2, 512], BF16) for i in range(2)]
                c_ln8 = sb(ph, 'mln8', [128, 1], F32)
                mng = sb(ph, 'mng', [128, 256], F32)
                S.dma('sp', mng[:], mng_d[:, l, :], writes=['mng'])

                S.dma('sp', QK[:, :, 0:256], m_q.rearrange("(c p) e -> p c e", p=128), writes=['mQKq'])
                S.dma('sp', QK[:, :, 256:512], m_k.rearrange("(c p) e -> p c e", p=128), writes=['mQKk'])
                S.dma('sp', V1[:], m_v1.rearrange("(c p) e -> p c e", p=128), writes=['mV1'])
                S.dma('sp', G16[:], m_g.rearrange("(c p) e -> p c e", p=128), writes=['mG'])
                S.op('dve', lambda e: e.memset(c_ln8[:], math.log(0.125)), writes=['mln8'])
                S.op('dve', lambda e: e.memset(CB[:], 0.0), writes=['mCB'])
                act(SP[:], G16[:, :, 8:16], AF.Exp, ['mG'], ['mSP'], scale=-1.0)
                act(SP[:], SP[:], AF.Ln, ['mSP', 'c_one'], ['mSP'], bias=c_one[:, 0:1])
                sp2 = SP[:].rearrange("p c r -> p (c r)")
                psF, pnF = next_ps()
                mm(psF[:, 0:NCH * 8], [(constf[:, 1, :], sp2)], pnF, ['mSP', 'constf'])
                psB, pnB = next_ps()
                mm(psB[:, 0:NCH * 8], [(constf[:, 2, :], sp2)], pnB, ['mSP', 'constf'])
                psF3 = psF[:, 0:NCH * 8].rearrange("p (c r) -> p c r", r=8)
                psB3 = psB[:, 0:NCH * 8].rearrange("p (c r) -> p c r", r=8)
                act(EQ[:, 0], psF3[:, :, 0:4], AF.Exp, [pnF], ['mEQ'], scale=-1.0)
                act(EQ[:, 1], psB3[:, :, 4:8], AF.Exp, [pnB], ['mEQ'], scale=-1.0)
                tt(TMPG[:], G16[:, :, 0:4], psF3[:, :, 0:4], ALU.add, ['mG', pnF], ['mTG'])
                act(EK[:, 0], TMPG[:], AF.Exp, ['mTG', 'mln8'], ['mEK'], bias=c_ln8[:, 0:1])
                tt(TMPG[:], G16[:, :, 4:8], psB3[:, :, 4:8], ALU.add, ['mG', pnB], ['mTG'])
                act(EK[:, 1], TMPG[:], AF.Exp, ['mTG', 'mln8'], ['mEK'], bias=c_ln8[:, 0:1])
                psD0, pnD0 = next_ps()
                mm(psD0[:, 0:NCH * 4], [(constf[:, 3, :], EQ[:, 0].rearrange("p c h -> p (c h)"))], pnD0, ['mEQ', 'constf'])
                psD1, pnD1 = next_ps()
                mm(psD1[:, 0:NCH * 4], [(constf[:, 4, :], EQ[:, 1].rearrange("p c h -> p (c h)"))], pnD1, ['mEQ', 'constf'])
                d03 = psD0[:, 0:NCH * 4].rearrange("p (c h) -> p c h", h=4)
                d13 = psD1[:, 0:NCH * 4].rearrange("p (c h) -> p c h", h=4)
                for hp in range(2):
                    for half in range(2):
                        psl = slice(half * 64, (half + 1) * 64)
                        hd = 2 * hp + half
                        cp(Dd[psl, 0, hp, :], d03[psl, :, hd], [pnD0], ['mD'])
                        cp(Dd[psl, 1, hp, 0:2], d13[psl, 0:2, hd][:, ::-1], [pnD1], ['mD'])
                        cp(Dd[psl, 1, hp, 2:NCH], d13[psl, 2:NCH, hd][:, ::-1], [pnD1], ['mD'])
                import os
                MST = int(os.environ.get('M_STOP', '9'))
                for s2 in range(0):
                    c0 = 2 + 2 * s2
                    rc = RCs[s2 % 2]; rs = RSs[s2 % 2]
                    S.dma('sp', rc[:], ropec_d[:, 2 * s2:2 * s2 + 2, :], writes=['mRC%d' % (s2 % 2)])
                    S.dma('sp', rs[:], ropes_d[:, 2 * s2:2 * s2 + 2, :], writes=['mRS%d' % (s2 % 2)])
                    qk4 = QK[:, c0:c0 + 2, :].rearrange("p c (h d) -> p c h d", d=64)
                    tt(t1[:].rearrange("p c (h d) -> p c h d", d=64), qk4, bc(rc[:], 1, 8), ALU.mult,
                       ['mQKq', 'mQKk', 'mRC%d' % (s2 % 2)], ['mt1'])
                    for cc in range(2):
                        qv = QK[:, c0 + cc, :].rearrange("p (h f a e) -> p h f a e", h=8, f=2, a=2, e=16)
                        tv = t2[:].rearrange("p (h f a e) -> p h f a e", h=8, f=2, a=2, e=16)
                        rv = rs[:, cc, :].rearrange("p (f a e) -> p f a e", f=2, a=2, e=16)
                        for a in range(2):
                            tt(tv[:, :, :, a, :], qv[:, :, :, 1 - a, :], bc(rv[:, :, a, :], 0, 8), ALU.mult,
                               ['mQKq', 'mQKk', 'mRS%d' % (s2 % 2)], ['mt2'])
                        tt(QK[:, c0 + cc, :], t1[:, cc, :], t2[:], ALU.add, ['mt1', 'mt2'], ['mQKq', 'mQKk'])

                def pos_of(d, c):
                    if d == 0:
                        return c
                    return 1 - c if c < 2 else NCH + 1 - c

                store_i = [0]
                for d in range(2 if MST >= 6 else (1 if MST >= 3 else 0)):
                    for s4 in range(9):
                        c0 = 4 * s4
                        n4 = min(4, NCH - c0)
                        tt(Qs[:, 0:n4, :].rearrange("p c (h e) -> p c h e", e=64),
                           QK[:, c0:c0 + n4, 0:256].rearrange("p c (h e) -> p c h e", e=64),
                           bc(EQ[:, d, c0:c0 + n4, :], 2, 64), ALU.mult, ['mQKq', 'mEQ'], ['mQs'])
                        tt(Ks[:, 0:n4, :].rearrange("p c (h e) -> p c h e", e=64),
                           QK[:, c0:c0 + n4, 256:512].rearrange("p c (h e) -> p c h e", e=64),
                           bc(EK[:, d, c0:c0 + n4, :], 2, 64), ALU.mult, ['mQKk', 'mEK'], ['mKs'])
                        for cc in range(n4):
                            c = c0 + cc
                            csl = slice(c * 128, (c + 1) * 128)
                            for i, (src, nm) in enumerate(((Qs, 'mQs'), (Qs, 'mQs'), (Ks, 'mKs'), (Ks, 'mKs'))):
                                hp = i % 2
                                S.op('pe', lambda e, i=i, src=src, cc=cc, hp=hp: e.transpose(
                                    out=PST[:, i * 128:(i + 1) * 128], in_=src[:, cc, hp * 128:(hp + 1) * 128], identity=identb),
                                    reads=[nm, 'cb'], writes=['pst'])
                            act(QsT[:, :, csl], PST[:, 0:256].rearrange("p (i t) -> p i t", t=128), AF.Copy, ['pst'], ['mQsT'])
                            act(KsT[:, :, csl], PST[:, 256:512].rearrange("p (i t) -> p i t", t=128), AF.Copy, ['pst'], ['mKsT'])
                            psP, pnP = next_ps()
                            for h in range(4):
                                pr, hp = (h % 2) * 64, h // 2
                                mm(psP[pr:pr + 64, hp * 65:(hp + 1) * 65],
                                   [(Ks[:, cc, h * 64:(h + 1) * 64], V1[:, c, h * 65:(h + 1) * 65])], pnP, ['mKs', 'mV1'])
                            pos = pos_of(d, c)
                            for hp in range(2):
                                ts(Cst[:, hp, :, pos], psP[:, hp * 65:(hp + 1) * 65], Dd[:, d, hp, pos:pos + 1], ALU.mult,
                                   [pnP, 'mD'], ['mC'])
                    if MST < 4:
                        continue
                    S.op('dve', lambda e, d=d: e.memset(Dd[:, d, :, 0:1], 0.0), reads=['mC'], writes=['mD'])
                    for hp in range(2):
                        cp(Dfull[:], bc(Dd[:, d, hp, :], 0, 65), ['mD'], ['mDf'])
                        cf = Cst[:, hp].rearrange("p e c -> p (e c)")
                        scan(cf, Dfull[:].rearrange("p e c -> p (e c)"), cf, 0.0, ['mDf', 'mC'], ['mC'])
                        for half in range(2):
                            psl = slice(half * 64, (half + 1) * 64)
                            cp(CB[psl, 2 * hp + half], Cst[psl, hp].rearrange("p e c -> p c e"), ['mC'], ['mCB'])
                    if MST < 5:
                        continue
                    if MST < 5:
                        continue
                    st3 = {}

                    def emit_S(c, d=d):
                        csl = slice(c * 128, (c + 1) * 128)
                        psS0, pnS0 = next_ps()
                        psS1, pnS1 = next_ps()
                        for h in range(4):
                            pr, hp = (h % 2) * 64, h // 2
                            bank, pnS = (psS0, pnS0) if h % 2 == 0 else (psS1, pnS1)
                            mm(bank[:, hp * 128:(hp + 1) * 128], [(KsT[pr:pr + 64, hp, csl], QsT[pr:pr + 64, hp, csl])], pnS, ['mKsT', 'mQsT'])
                        spb = SpT[c % 2]; spn = 'mSpT%d' % (c % 2)
                        tt(spb[:, 0:4:2, :], psS0[:, 0:256].rearrange("p (h t) -> p h t", t=128), bc(cb[:, 1 + d, :], 0, 2), ALU.mult,
                           [pnS0, 'cb'], [spn])
                        tt(spb[:, 1:4:2, :], psS1[:, 0:256].rearrange("p (h t) -> p h t", t=128), bc(cb[:, 1 + d, :], 0, 2), ALU.mult,
                           [pnS1, 'cb'], [spn])

                    def emit_H(c, d=d):
                        csl = slice(c * 128, (c + 1) * 128)
                        pos = pos_of(d, c)
                        b2 = c % 2
                        sfx = str(b2)
                        spb = SpT[b2]; spn = 'mSpT%d' % b2
                        dd, rr, hb, hs, sq2, ss, lnn, rsn, y1, YM = (dd2[b2], rr2[b2], hb2[b2], hs2[b2], sq22[b2], ss2[b2], lnn2[b2],
                                                                    rsn2[b2], y12[b2], YM2[b2])
                        psH, pnH = next_ps()
                        for h in range(4):
                            hp = h // 2
                            pairs = [(spb[:, h, :], V1[:, c, h * 65:(h + 1) * 65])]
                            if pos > 0:
                                pairs.append((QsT[:, hp, csl], CB[:, h, pos - 1, :]))
                            mm(psH[:, h * 65:(h + 1) * 65], pairs, pnH, [spn, 'mV1', 'mQsT', 'mCB'])
                        H3 = psH[:, 0:260].rearrange("p (h e) -> p h e", e=65)
                        act(dd[:], H3[:, :, 64], AF.Abs, [pnH], ['mdd' + sfx])
                        ts(dd[:], dd[:], 1.0, ALU.max, ['mdd' + sfx], ['mdd' + sfx])
                        S.op('dve', lambda e: e.reciprocal(out=rr[:], in_=dd[:]), reads=['mdd' + sfx], writes=['mrr' + sfx])
                        if d == 0:
                            tt(HFs[:, c, :].rearrange("p (h e) -> p h e", e=64), H3[:, :, 0:64], bc(rr[:], 1, 64), ALU.mult,
                               [pnH, 'mrr' + sfx], ['mHF'])
                            return
                        tt(hb[:].rearrange("p (h e) -> p h e", e=64), H3[:, :, 0:64], bc(rr[:], 1, 64), ALU.mult, [pnH, 'mrr' + sfx], ['mhb' + sfx])
                        tt(HFs[:, c, :], hb[:], HFs[:, c, :], ALU.add, ['mhb' + sfx, 'mHF'], ['mHF'], eng='pool')

                    def emit_B(g, d=d):
                        c0 = 4 * g
                        n4 = min(4, NCH - c0)
                        b2 = g % 2
                        sfx = 'B%d' % b2
                        sob = SOs[b2]; son = 'mSO%d' % b2
                        S.dma('sp', sob[:, 0:n4, :], m_o[c0 * 128:(c0 + n4) * 128, :].rearrange("(c p) e -> p c e", p=128), writes=[son])
                        sqb = (Qs, Ks)[b2]; sqn = ('mQs', 'mKs')[b2]
                        y1b = Dfull[:].rearrange("p e c -> p (e c)")[:, b2 * 1024:(b2 + 1) * 1024]
                        y1n = 'mDf%d' % b2
                        ymb = YMB[b2]; ymn = 'mYMB%d' % b2
                        ssb, lnb, rsb = ssB[b2], lnB[b2], rsB[b2]
                        HS = HFs[:, c0:c0 + n4, :]
                        tt(sqb[:, 0:n4, :], HS, HS, ALU.mult, ['mHF'], [sqn], eng='pool')
                        S.op('dve', lambda e: e.tensor_reduce(out=ssb[:, 0:n4 * 4], in_=sqb[:, 0:n4, :].rearrange("p c (h e) -> p (c h) e", e=64),
                                                              axis=AX.X, op=ALU.add), reads=[sqn], writes=['mss' + sfx])
                        act(lnb[:, 0:n4 * 4], ssb[:, 0:n4 * 4], AF.Ln, ['mss' + sfx, 'c_eps'], ['mln' + sfx], scale=1.0 / HD, bias=c_eps[:, 0:1])
                        act(rsb[:, 0:n4 * 4], lnb[:, 0:n4 * 4], AF.Exp, ['mln' + sfx], ['mrs' + sfx], scale=-0.5)
                        y3 = y1b[:, 0:n4 * 256].rearrange("p (g e) -> p g e", e=64)
                        tt(y3, HS.rearrange("p c (h e) -> p (c h) e", e=64), bc(rsb[:, 0:n4 * 4], 1, 64), ALU.mult, ['mHF', 'mrs' + sfx], [y1n])
                        y4 = y1b[:, 0:n4 * 256].rearrange("p (c f) -> p c f", f=256)
                        tt(y4, y4, bc(mng[:], 0, n4), ALU.mult, [y1n, 'mng'], [y1n])
                        tt(ymb[:, 0:n4, :], y4, sob[:, 0:n4, :], ALU.mult, [y1n, son], [ymn], eng='pool')
                        for cc in range(n4):
                            for hp in range(2):
                                S.op('pe', lambda e, cc=cc, hp=hp, ymb=ymb: e.transpose(
                                    out=PST[:, (cc * 2 + hp) * 128:(cc * 2 + hp + 1) * 128], in_=ymb[:, cc, hp * 128:(hp + 1) * 128], identity=identb),
                                    reads=[ymn, 'cb'], writes=['pst'])
                        mb = MIXs[b2]; mn = 'mMIX%d' % b2
                        act(mb[:, :, 0:n4 * 128].rearrange("p i (c t) -> p c i t", t=128),
                            PST[:, 0:n4 * 256].rearrange("p (c i t) -> p c i t", i=2, t=128), AF.Copy, ['pst'], [mn])
                        S.dma('pool', mixT[0:256, c0 * 128:(c0 + n4) * 128].rearrange("(i p) t -> p i t", p=128),
                              mb[:, :, 0:n4 * 128], reads=[mn])

                    emit_S(0)
                    for c in range(NCH):
                        if c + 1 < NCH:
                            emit_S(c + 1)
                        emit_H(c)
                        if d == 1 and (c % 4 == 3 or c == NCH - 1):
                            emit_B(c // 4)
                S.flush()


        def zero_rows(r0, r1):
            with contextlib.ExitStack() as ph:
                z = sb(ph, 'zz', [128, NTOK], BF16)
                S.op('dve', lambda e: e.memset(z[:], 0.0), writes=['zz'])
                for i in range(r0, r1):
                    S.dma('sp', mixT[i * 128:(i + 1) * 128, :], z[:], reads=['zz'])
                S.flush()

        import os
        PH = os.environ.get('K_PHASES', 'ALMNC')
        for l in range(L):
            phase_A(l)
            if stop_after == ('A', l):
                break
            if 'L' in PH:
                phase_lru(l)
            if 'M' in PH:
                phase_mlstm(l)
            else:
                zero_rows(0, 2)
            if 'N' in PH:
                phase_na(l)
            else:
                zero_rows(2, 6)
            if 'C' in PH:
                phase_C(l)
        S.flush()
    return nc


_NC_CACHE = {}


def kernel(**inputs):
    depth = inputs['w_in'].shape[0]
    maps = prep_inputs(inputs, depth)
    if depth not in _NC_CACHE:
        _NC_CACHE[depth] = build(depth)
    nc = _NC_CACHE[depth]
    res = run_bass_kernel_spmd(nc, maps, core_ids=list(range(len(maps))))
    out = np.stack([np.ascontiguousarray(r['outT'].T) for r in res.results], axis=0)
    return out.astype(np.float32)
```
